# Optimizing a Trainium2 kernel written in Bass

```python
import math
import jax, jax.numpy as jnp
from jax import lax
import numpy as np

D_MODEL = 1024
BATCH = 8
SEQ = 4096
DEPTH = 2

HEAD_DIM = 64
N_HEADS_A = D_MODEL // HEAD_DIM
MOBA_BLOCK = 256
MOBA_TOPK = 3
MOBA_Q_CHUNK = 16
N_HEADS_B = D_MODEL // HEAD_DIM
N_KV_B = max(1, N_HEADS_B // 8)
WINDOW = 128
D_FF = 256 * math.ceil(8 * D_MODEL / 3 / 256)
N_EXPERTS = 8
MOE_TOPK = 2
D_FF_EXPERT = 7 * D_MODEL // 2
N_A = max(1, DEPTH // 2)
N_B = DEPTH - N_A
N_DENSE = (DEPTH + 1) // 2
N_MOE = DEPTH // 2
DEEPNORM_ALPHA = (2 * DEPTH) ** 0.25
DEEPNORM_BETA = (8 * DEPTH) ** -0.25
LN_EPS = 1e-5
NEG_INF = -1e30

kernel_name = 'hybrid_moba_swa_sinks_yoco_deepnorm_moe'


def layer_norm(x, g, b):
    xf = x.astype(jnp.float32)
    mu = xf.mean(-1, keepdims=True)
    var = jnp.square(xf - mu).mean(-1, keepdims=True)
    return ((xf - mu) * lax.rsqrt(var + LN_EPS) * g.astype(jnp.float32) + b.astype(jnp.float32)).astype(x.dtype)


def alibi_slopes(n):
    return jnp.exp2(-8.0 * jnp.arange(1, n + 1, dtype=jnp.float32) / n)


def moba_attention(x, w_qkv, w_o):
    B, T, _ = x.shape
    H, hd, BS, QC = N_HEADS_A, HEAD_DIM, MOBA_BLOCK, MOBA_Q_CHUNK
    q, k, v = jnp.split(x @ w_qkv, 3, axis=-1)
    Tp = -(-T // BS) * BS
    pad = Tp - T

    def heads(t):
        t = jnp.pad(t, ((0, 0), (0, pad), (0, 0)))
        return t.reshape(B, Tp, H, hd).transpose(0, 2, 1, 3)

    q, k, v = heads(q), heads(k), heads(v)
    NB = Tp // BS
    k_blocks = k.reshape(B, H, NB, BS, hd)
    v_blocks = v.reshape(B, H, NB, BS, hd)
    k_mean = k_blocks.astype(jnp.float32).mean(axis=3)
    gate = jnp.einsum('bhtd,bhnd->bhtn', q.astype(jnp.float32), k_mean)
    q_blk = jnp.arange(Tp) // BS
    past = jnp.arange(NB)[None, :] < q_blk[:, None]
    gate = jnp.where(past, gate, NEG_INF)
    k_sel = min(MOBA_TOPK, NB)
    _, sel = lax.top_k(gate, k_sel)
    NC = Tp // QC
    q_c = (q * hd ** -0.5).reshape(B, H, NC, QC, hd).transpose(2, 0, 1, 3, 4)
    sel_c = sel.reshape(B, H, NC, QC, k_sel).transpose(2, 0, 1, 3, 4)
    slopes = alibi_slopes(H)
    pos_in_blk = jnp.arange(BS)
    gather = jax.vmap(jax.vmap(lambda blocks, ix: blocks[ix]))

    def chunk(args):
        c, qc, ix = args
        t = c * QC + jnp.arange(QC)
        blk = (c * QC) // BS
        k_own = lax.dynamic_index_in_dim(k_blocks, blk, axis=2, keepdims=False)
        v_own = lax.dynamic_index_in_dim(v_blocks, blk, axis=2, keepdims=False)
        d_own = (t[:, None] - (blk * BS + pos_in_blk)[None, :]).astype(jnp.float32)
        l_own = jnp.einsum('bhqd,bhsd->bhqs', qc, k_own).astype(jnp.float32) - slopes[None, :, None, None] * d_own
        l_own = jnp.where(d_own >= 0, l_own, NEG_INF)
        kg = gather(k_blocks, ix)
        vg = gather(v_blocks, ix)
        d_sel = (t[None, None, :, None, None] - (ix[..., None] * BS + pos_in_blk)).astype(jnp.float32)
        l_sel = jnp.einsum('bhqd,bhqnsd->bhqns', qc, kg).astype(jnp.float32) - slopes[None, :, None, None, None] * d_sel
        l_sel = jnp.where((ix < blk)[..., None], l_sel, NEG_INF).reshape(B, H, QC, k_sel * BS)
        p = jax.nn.softmax(jnp.concatenate([l_own, l_sel], axis=-1), axis=-1)
        p_own = p[..., :BS].astype(v.dtype)
        p_sel = p[..., BS:].reshape(B, H, QC, k_sel, BS).astype(v.dtype)
        return jnp.einsum('bhqs,bhsd->bhqd', p_own, v_own) + jnp.einsum('bhqns,bhqnsd->bhqd', p_sel, vg)

    o = lax.map(chunk, (jnp.arange(NC), q_c, sel_c))
    o = o.transpose(1, 0, 3, 2, 4).reshape(B, Tp, H * hd)[:, :T]
    return o @ w_o


def swa_sinks_attention(x, w_q, w_o, sinks, k, v):
    B, T, _ = x.shape
    H, KV, hd, W = N_HEADS_B, N_KV_B, HEAD_DIM, WINDOW
    G = H // KV
    NB = T // W
    q = (x @ w_q).reshape(B, NB, W, KV, G, hd) * hd ** -0.5

    def band(t):
        tb = t.reshape(B, NB, W, KV, hd)
        prev = jnp.pad(tb, ((0, 0), (1, 0), (0, 0), (0, 0), (0, 0)))[:, :-1]
        return jnp.concatenate([prev, tb], axis=2)

    kb, vb = band(k), band(v)
    logits = jnp.einsum('bnqkgd,bnskd->bkgnqs', q, kb).astype(jnp.float32)
    s_idx = jnp.arange(2 * W)
    d = W + jnp.arange(W)[:, None] - s_idx[None, :]
    slopes = alibi_slopes(H).reshape(KV, G)[None, :, :, None, None, None]
    logits = logits - slopes * d.astype(jnp.float32)
    key_pos = jnp.arange(NB)[:, None, None] * W - W + s_idx[None, None, :]
    mask = ((d >= 0) & (d < W))[None] & (key_pos >= 0)
    logits = jnp.where(mask, logits, NEG_INF)
    sink = jnp.broadcast_to(sinks.reshape(KV, G).astype(jnp.float32)[None, :, :, None, None, None],
                            logits.shape[:-1] + (1,))
    p = jax.nn.softmax(jnp.concatenate([logits, sink], axis=-1), axis=-1)[..., :-1]
    o = jnp.einsum('bkgnqs,bnskd->bnqkgd', p.astype(v.dtype), vb).reshape(B, T, H * hd)
    return o @ w_o


def swiglu(x, w_gate, w_up, w_down):
    return (jax.nn.silu(x @ w_gate) * (x @ w_up)) @ w_down


def moe_swiglu(x, w_router, w_gate, w_up, w_down):
    B, T, D = x.shape
    xt = x.reshape(B * T, D)
    logits = (xt @ w_router).astype(jnp.float32)
    top_val, top_idx = lax.top_k(logits, MOE_TOPK)
    top_w = jax.nn.softmax(top_val, axis=-1)
    gates = jnp.sum(jax.nn.one_hot(top_idx, N_EXPERTS, dtype=jnp.float32) * top_w[..., None], axis=1)
    y = jnp.zeros_like(xt)
    for e in range(N_EXPERTS):
        y = y + gates[:, e:e + 1].astype(x.dtype) * swiglu(xt, w_gate[e], w_up[e], w_down[e])
    return y.reshape(B, T, D)


def setup_inputs(seed: int = 0) -> dict:
    key = jax.random.key(seed)
    ks = jax.random.split(key, 20)
    D = D_MODEL
    HA = N_HEADS_A * HEAD_DIM
    HB = N_HEADS_B * HEAD_DIM
    KVB = N_KV_B * HEAD_DIM
    beta = DEEPNORM_BETA

    def nrm(k, shape, fan_in, scale=1.0):
        return jax.random.normal(k, shape, jnp.float32) * (scale * fan_in ** -0.5)

    x = jax.random.normal(ks[0], (BATCH, SEQ, D), jnp.float32)
    w_qkv_a = jnp.concatenate([nrm(ks[1], (N_A, D, 2 * HA), D), nrm(ks[2], (N_A, D, HA), D, beta)], axis=-1)
    w_o_a = nrm(ks[3], (N_A, HA, D), HA, beta)
    w_kv_shared = jnp.concatenate([nrm(ks[4], (D, KVB), D), nrm(ks[5], (D, KVB), D, beta)], axis=-1)
    w_q_b = nrm(ks[6], (N_B, D, HB), D)
    w_o_b = nrm(ks[7], (N_B, HB, D), HB, beta)
    sinks_b = 0.5 * jax.random.normal(ks[8], (N_B, N_HEADS_B), jnp.float32)
    w_gate_d = nrm(ks[9], (N_DENSE, D, D_FF), D, beta)
    w_up_d = nrm(ks[10], (N_DENSE, D, D_FF), D, beta)
    w_down_d = nrm(ks[11], (N_DENSE, D_FF, D), D_FF, beta)
    w_router = nrm(ks[12], (N_MOE, D, N_EXPERTS), D)
    w_gate_e = nrm(ks[13], (N_MOE, N_EXPERTS, D, D_FF_EXPERT), D, beta)
    w_up_e = nrm(ks[14], (N_MOE, N_EXPERTS, D, D_FF_EXPERT), D, beta)
    w_down_e = nrm(ks[15], (N_MOE, N_EXPERTS, D_FF_EXPERT, D), D_FF_EXPERT, beta)
    ln_gain = 1.0 + 0.05 * jax.random.normal(ks[16], (DEPTH, 2, D), jnp.float32)
    ln_bias = 0.02 * jax.random.normal(ks[17], (DEPTH, 2, D), jnp.float32)
    return {'x': x, 'w_qkv_a': w_qkv_a, 'w_o_a': w_o_a, 'w_kv_shared': w_kv_shared,
            'w_q_b': w_q_b, 'w_o_b': w_o_b, 'sinks_b': sinks_b,
            'w_gate_d': w_gate_d, 'w_up_d': w_up_d, 'w_down_d': w_down_d,
            'w_router': w_router, 'w_gate_e': w_gate_e, 'w_up_e': w_up_e, 'w_down_e': w_down_e,
            'ln_gain': ln_gain, 'ln_bias': ln_bias}


def reference(x, w_qkv_a, w_o_a, w_kv_shared, w_q_b, w_o_b, sinks_b,
              w_gate_d, w_up_d, w_down_d, w_router, w_gate_e, w_up_e, w_down_e,
              ln_gain, ln_bias):
    B, T, _ = x.shape
    alpha = DEEPNORM_ALPHA
    k_sh = v_sh = None
    for i in range(DEPTH):
        if i < N_A:
            mix = moba_attention(x, w_qkv_a[i], w_o_a[i])
        else:
            if i == N_A:
                kv = (x @ w_kv_shared).reshape(B, T, 2, N_KV_B, HEAD_DIM)
                k_sh, v_sh = kv[:, :, 0], kv[:, :, 1]
            j = i - N_A
            mix = swa_sinks_attention(x, w_q_b[j], w_o_b[j], sinks_b[j], k_sh, v_sh)
        x = layer_norm(alpha * x + mix, ln_gain[i, 0], ln_bias[i, 0])
        if i % 2 == 0:
            ffn = swiglu(x, w_gate_d[i // 2], w_up_d[i // 2], w_down_d[i // 2])
        else:
            ffn = moe_swiglu(x, w_router[i // 2], w_gate_e[i // 2], w_up_e[i // 2], w_down_e[i // 2])
        x = layer_norm(alpha * x + ffn, ln_gain[i, 1], ln_bias[i, 1])
    return x
```

```python
import math
from contextlib import ExitStack

import numpy as np
import ml_dtypes

import concourse.bass as bass
import concourse.mybir as mybir
from concourse.bass_utils import run_bass_kernel_spmd

F32 = mybir.dt.float32
BF16 = mybir.dt.bfloat16
AF = mybir.ActivationFunctionType
ALU = mybir.AluOpType
AX = mybir.AxisListType

D = 1024
H = 16
HD = 64
NE = 8
NEG = -30000.0
BIGNEG = -1.0e9
ALPHA = 4.0 ** 0.25
LN_EPS = 1e-5

SEM_LIMIT = 20000
N_DMA_SEMS = 16


class Res:
    __slots__ = ("writers", "readers", "excl")

    def __init__(self, excl=False):
        self.writers = []
        self.readers = []
        self.excl = excl


class Op:
    __slots__ = ("eng", "fn", "deps", "signal", "sem_key", "sem_val", "is_dma")

    def __init__(self, eng, fn, is_dma):
        self.eng = eng
        self.fn = fn
        self.deps = []
        self.signal = False
        self.sem_key = None
        self.sem_val = None
        self.is_dma = is_dma


class Prog:
    ENGS = ("pe", "act", "dve", "pool", "sp")

    def __init__(self, nc):
        self.nc = nc
        self.ops = []
        self.dma_count = {e: 0 for e in self.ENGS}
        self.dma_last = {}
        self.last_compute = {}
        self.pending = {}
        self.dead = False

    def _record(self, op, reads, writes):
        deps = {}
        for r in reads:
            seen = set()
            for w in reversed(r.writers):
                if not w.is_dma:
                    if w.eng in seen:
                        continue
                    seen.add(w.eng)
                deps[id(w)] = (w, "raw")
            if r.excl and op.eng in ("act", "dve"):
                for rd in r.readers:
                    if rd.eng != op.eng and rd.eng in ("act", "dve"):
                        deps[id(rd)] = (rd, "raw")
        for w in writes:
            for ww in w.writers:
                if id(ww) not in deps:
                    deps[id(ww)] = (ww, "waw")
            for rd in w.readers:
                if id(rd) not in deps:
                    deps[id(rd)] = (rd, "war")
        for p, kind in deps.values():
            if p is op:
                continue
            if (not p.is_dma) and (not op.is_dma) and p.eng == op.eng:
                if op.eng == "pe" or kind == "waw":
                    continue
            if p.is_dma and op.is_dma and p.eng == op.eng and kind == "waw":
                continue
            op.deps.append(p)
        pend = self.pending.get(op.eng)
        if pend:
            for p in pend:
                if p is not op and p not in op.deps:
                    op.deps.append(p)
            self.pending[op.eng] = None
        for r in reads:
            r.readers.append(op)
        for w in writes:
            if w.readers:
                w.writers = [op]
                w.readers = []
            else:
                w.writers.append(op)
                if len(w.writers) > 40:
                    w.writers = w.writers[-40:]
        if not op.is_dma:
            self.last_compute[op.eng] = op
        self.ops.append(op)
        return op

    def op(self, eng, fn, reads=(), writes=()):
        if self.dead:
            return None
        return self._record(Op(eng, fn, False), reads, writes)

    def dma(self, eng, out_ap, in_ap, reads=(), writes=(), fn=None):
        if fn is None:
            def fn(e, out_ap=out_ap, in_ap=in_ap):
                return e.dma_start(out=out_ap, in_=in_ap)

        if self.dead:
            return None
        op = Op(eng, fn, True)
        k = self.dma_count[eng]
        self.dma_count[eng] = k + 1
        slot = k % N_DMA_SEMS
        op.sem_key = ("dma", eng, slot)
        op.sem_val = 16 * (k // N_DMA_SEMS + 1)
        op.signal = True
        prev = self.dma_last.get((eng, slot))
        self._record(op, reads, writes)
        if prev is not None and prev not in op.deps:
            op.deps.append(prev)
        self.dma_last[(eng, slot)] = op
        return op

    def barrier(self):
        lasts = list(self.last_compute.values()) + list(self.dma_last.values())
        for e in self.ENGS:
            self.pending[e] = list(lasts)

    def emit(self, final_eng="sp"):
        nc = self.nc
        for op in self.ops:
            for p in op.deps:
                if not p.is_dma:
                    p.signal = True
        cnt = {e: 0 for e in self.ENGS}
        epoch = {e: 0 for e in self.ENGS}
        keys = set()
        for op in self.ops:
            if op.is_dma:
                keys.add(op.sem_key)
                continue
            if op.signal:
                if cnt[op.eng] >= SEM_LIMIT:
                    cnt[op.eng] = 0
                    epoch[op.eng] += 1
                cnt[op.eng] += 1
                op.sem_key = ("eng", op.eng, epoch[op.eng])
                op.sem_val = cnt[op.eng]
                keys.add(op.sem_key)
        by_eng = {e: [] for e in self.ENGS}
        for op in self.ops:
            by_eng[op.eng].append(op)
        final = {}
        for op in self.ops:
            if op.is_dma:
                final[op.sem_key] = max(final.get(op.sem_key, 0), op.sem_val)
        with ExitStack() as es:
            sems = {}
            for key in sorted(keys, key=str):
                sems[key] = es.enter_context(nc.semaphore("s_" + "_".join(str(x) for x in key)))
            block = es.enter_context(nc.Block())

            def run(engname, e):
                known = {}
                for op in by_eng[engname]:
                    need = {}
                    for p in op.deps:
                        if need.get(p.sem_key, 0) < p.sem_val:
                            need[p.sem_key] = p.sem_val
                    for key, val in need.items():
                        if known.get(key, 0) >= val:
                            continue
                        e.wait_ge(sems[key], val)
                        known[key] = val
                    ins = op.fn(e)
                    if op.signal:
                        ins.then_inc(sems[op.sem_key], 16 if op.is_dma else 1)
                if engname == final_eng:
                    for key, val in final.items():
                        if known.get(key, 0) < val:
                            e.wait_ge(sems[key], val)

            @block.tensor
            def _(e):
                run("pe", e)

            @block.scalar
            def _(e):
                run("act", e)

            @block.vector
            def _(e):
                run("dve", e)

            @block.gpsimd
            def _(e):
                run("pool", e)

            @block.sync
            def _(e):
                run("sp", e)
        return len(self.ops)


ARENA_COLS = 103000


class Deferred:
    def __init__(self, la):
        self.q = []
        self.la = la

    def push(self, fn):
        self.q.append(fn)

    def step(self):
        while len(self.q) > self.la:
            self.q.pop(0)()

    def flush(self):
        while self.q:
            self.q.pop(0)()


class Delayed:
    def __init__(self):
        self.q = []

    def push(self, fn, delay):
        self.q.append([delay, fn])

    def tick(self):
        for it in self.q:
            it[0] -= 1
        while self.q and self.q[0][0] <= 0:
            self.q.pop(0)[1]()

    def flush(self):
        while self.q:
            self.q.pop(0)[1]()


class KB:
    def cut(self, n):
        if self.cfg.get("cut") == n:
            self.P.dead = True

    def __init__(self, nc, cfg):
        self.nc = nc
        self.cfg = cfg
        self.P = Prog(nc)
        self.off = 0
        self.persist = 0

    def alloc(self, cols, dt):
        n = cols * 2 if dt == F32 else cols
        n = (n + 15) // 16 * 16
        a = self.arena[:, self.off:self.off + n]
        self.off += n
        assert self.off <= ARENA_COLS, f"SBUF arena overflow {self.off}"
        if dt == F32:
            return a.bitcast(F32)[:, 0:cols]
        return a[:, 0:cols]

    def release(self):
        self.off = self.persist
        self.P.barrier()

    def mm(self, out, lhsT, rhs, start, stop, rd, wr):
        self.P.op("pe", lambda e: e.matmul(out, lhsT, rhs, start=start, stop=stop), rd, wr)

    def tr(self, out, in_, ident, rd, wr):
        self.P.op("pe", lambda e: e.transpose(out, in_, ident), rd, wr)

    def act(self, out, in_, func, rd, wr, scale=1.0, bias=None):
        if bias is None:
            self.P.op("act", lambda e: e.activation(out=out, in_=in_, func=func, scale=scale), rd, wr)
        else:
            self.P.op("act", lambda e: e.activation(out=out, in_=in_, func=func, bias=bias, scale=scale), rd, wr)

    def cp(self, eng, out, in_, rd, wr):
        if eng == "act":
            self.P.op("act", lambda e: e.activation(out=out, in_=in_, func=AF.Copy), rd, wr)
        else:
            self.P.op(eng, lambda e: e.tensor_copy(out=out, in_=in_), rd, wr)

    def tt(self, eng, out, in0, in1, op, rd, wr):
        self.P.op(eng, lambda e: e.tensor_tensor(out=out, in0=in0, in1=in1, op=op), rd, wr)

    def ts(self, eng, out, in0, s1, s2, op0, op1, rd, wr):
        if s2 is None:
            self.P.op(eng, lambda e: e.tensor_scalar(out=out, in0=in0, scalar1=s1, scalar2=None, op0=op0), rd, wr)
        else:
            self.P.op(eng, lambda e: e.tensor_scalar(out=out, in0=in0, scalar1=s1, scalar2=s2, op0=op0, op1=op1), rd, wr)

    def stt(self, eng, out, in0, scalar, in1, op0, op1, rd, wr):
        self.P.op(eng, lambda e: e.scalar_tensor_tensor(out=out, in0=in0, scalar=scalar, in1=in1, op0=op0, op1=op1), rd, wr)

    def red(self, out, in_, op, rd, wr):
        self.P.op("dve", lambda e: e.tensor_reduce(out=out, in_=in_, axis=AX.X, op=op), rd, wr)

    def recip(self, out, in_, rd, wr):
        self.P.op("dve", lambda e: e.reciprocal(out=out, in_=in_), rd, wr)

    def memset(self, eng, ap, val, wr):
        self.P.op(eng, lambda e: e.memset(ap, val), (), wr)

    def dma(self, eng, out, in_, rd=(), wr=()):
        self.P.dma(eng, out, in_, rd, wr)

    def gather(self, out, table, idx, rd=(), wr=()):
        def fn(e):
            return e.indirect_dma_start(out=out, out_offset=None, in_=table,
                                        in_offset=bass.IndirectOffsetOnAxis(ap=idx, axis=0))
        self.P.dma("pool", None, None, rd, wr, fn=fn)

    def scatter(self, table, idx, in_, rd=(), wr=()):
        def fn(e):
            return e.indirect_dma_start(out=table, out_offset=bass.IndirectOffsetOnAxis(ap=idx, axis=0),
                                        in_=in_, in_offset=None)
        self.P.dma("pool", None, None, rd, wr, fn=fn)

    def layernorm(self, y, r_y, gain_bc, bias_bc, r_gb, out, r_out, tmp):
        st, mv, sd, rstd, xn, r_t = tmp
        self.P.op("dve", lambda e: e.bn_stats(out=st[:, 0:6], in_=y[:, 0:512]), [r_y], [r_t[0]])
        self.P.op("dve", lambda e: e.bn_stats(out=st[:, 6:12], in_=y[:, 512:1024]), [r_y], [r_t[0]])
        self.P.op("dve", lambda e: e.bn_aggr(out=mv, in_=st.rearrange("p (a b) -> p a b", b=6)), [r_t[0]], [r_t[1]])
        self.act(sd, mv[:, 1:2], AF.Sqrt, [r_t[1]], [r_t[2]], bias=LN_EPS)
        self.recip(rstd, sd, [r_t[2]], [r_t[3]])
        self.stt("dve", sd, mv[:, 0:1], -1.0, rstd, ALU.mult, ALU.mult, [r_t[1], r_t[3], r_t[2]], [r_t[2]])
        self.act(xn, y, AF.Identity, [r_y, r_t[2], r_t[3]], [r_t[4]], scale=rstd[:, 0:1], bias=sd[:, 0:1])
        self.tt("dve", xn, xn, gain_bc, ALU.mult, [r_t[4], r_gb], [r_t[4]])
        self.tt("dve", out, xn, bias_bc, ALU.add, [r_t[4], r_gb], [r_out])

    def ln_tmp(self):
        st = self.alloc(12, F32)
        mv = self.alloc(2, F32)
        sd = self.alloc(1, F32)
        rstd = self.alloc(1, F32)
        xn = self.alloc(D, F32)
        return (st, mv, sd, rstd, xn, [Res() for _ in range(5)])


def build(cfg):
    T = cfg["T"]
    FF = cfg["FF"]
    FE = cfg["FE"]
    dbg = cfg.get("debug", False)
    phases = cfg.get("phases", "ABCD")
    NT, NG, NB = T // 128, T // 512, T // 256
    assert T % 512 == 0 and NT * NB <= 512
    nc = bass.Bass("TRN2", target_bir_lowering=False)

    def din(name, shape):
        return nc.dram_tensor(name, shape, F32, kind="ExternalInput").ap()

    def dscr(name, shape, dt):
        return nc.dram_tensor(name, shape, dt, kind=("ExternalOutput" if dbg else "Internal")).ap()

    x = din("x", [T, D])
    w_qkv = din("w_qkv", [D, 3 * D])
    w_o_a = din("w_o_a", [D, D])
    w_kv = din("w_kv", [D, 256])
    w_q_b = din("w_q_b", [D, D])
    w_o_b = din("w_o_b", [D, D])
    sinks = din("sinks", [1, H])
    w_gate_d = din("w_gate_d", [D, FF])
    w_up_d = din("w_up_d", [D, FF])
    w_down_d = din("w_down_d", [FF, D])
    w_router = din("w_router", [D, NE])
    w_gate_e = din("w_gate_e", [NE, D, FE])
    w_up_e = din("w_up_e", [NE, D, FE])
    w_down_e = din("w_down_e", [NE, FE, D])
    ln_gain = din("ln_gain", [4, D])
    ln_bias = din("ln_bias", [4, D])
    c_ident = din("c_ident", [128, 128])
    c_tri = din("c_tri", [128, 512])
    c_tri2 = din("c_tri2", [128, 512])
    c_kaug = din("c_kaug", [H, 32, T])
    c_qaug = din("c_qaug", [H, 32, T])
    c_past = din("c_past", [128, NT * NB])
    c_own = din("c_own", [128, NT * NB])
    c_kaug2 = din("c_kaug2", [2, 32, 128])
    c_qaug2 = din("c_qaug2", [H, 32, 128])
    S, KMAX, NSLOT = slot_geom(T)
    NSCR = FE // 512
    c_lstrict = din("c_lstrict", [128, 128])
    c_ones = din("c_ones", [128, 128])
    c_thr = din("c_thr", [128, NE * KMAX])
    c_sidx = din("c_sidx", [128, NSLOT * NE])
    c_gp = din("c_gp", [128, 8 * NSCR])
    c_dp = din("c_dp", [128, NSCR * 4])
    out = nc.dram_tensor("out", [T, D], F32, kind="ExternalOutput").ap()

    attnT = dscr("attnT", [D, T], BF16)
    x1s = dscr("x1s", [T, D], F32)
    x1Ts = dscr("x1Ts", [NT, 128, D], BF16)
    x2s = dscr("x2s", [T, D], F32)
    x2Ts = dscr("x2Ts", [NT, 128, D], BF16)
    x3s = dscr("x3s", [T, D], F32)
    x3Ts = dscr("x3Ts", [NT, 128, D], BF16)
    gates_d = dscr("gates_d", [128, NT * NE], F32)
    x3bs = dscr("x3bs", [T, D], BF16)

    with ExitStack() as es:
        kb = KB(nc, cfg)
        kb.arena = es.enter_context(nc.sbuf_tensor("arena", [128, ARENA_COLS], BF16))
        ps = [es.enter_context(nc.psum_tensor(f"ps{i}", [128, 512], F32)) for i in range(8)]
        psb = [p.bitcast(BF16) for p in ps]
        r_ps = [Res(excl=True) for _ in range(8)]
        P = kb.P
        alloc = kb.alloc

        ident = alloc(128, BF16)
        identf = alloc(128, F32)
        tri = alloc(512, BF16)
        tri2 = alloc(512, BF16)
        onesf = alloc(64, F32)
        gates = alloc(NT * NE, F32)
        E1s = alloc(NT * NE, F32)
        E2s = alloc(NT * NE, F32)
        W1s = alloc(NT, F32)
        W2s = alloc(NT, F32)
        r_route = Res()
        r_c = Res()
        r_gates = Res()
        kb.dma("pool", ident, c_ident, wr=[r_c])
        kb.dma("sp", identf, c_ident, wr=[r_c])
        kb.dma("pool", tri, c_tri, wr=[r_c])
        kb.dma("pool", tri2, c_tri2, wr=[r_c])
        kb.memset("pool", onesf, 1.0, [r_c])
        xsorted = dscr("xsorted", [NSLOT * S, D], BF16)
        ysd = dscr("ysd", [NSLOT * S, D], F32)
        r_xso = Res()
        if "D" in phases and cfg.get("moe", "routed") == "routed":
            zt = alloc(D, BF16)
            r_zt = Res()
            kb.memset("pool", zt, 0.0, [r_zt])
        kb.persist = kb.off

        def load_gb(idx):
            g = alloc(D, F32)
            b = alloc(D, F32)
            r = Res()
            kb.dma("sp", g, ln_gain[idx:idx + 1, :].to_broadcast([128, D]), wr=[r])
            kb.dma("sp", b, ln_bias[idx:idx + 1, :].to_broadcast([128, D]), wr=[r])
            return g, b, r

        def t_cast(src_f32, r_src, xb, r_xb):
            kb.cp("act", xb, src_f32, [r_src], [r_xb])

        def to_T_block(src_f32, r_src, dst_dram, bank, xb, r_xb, xTt, r_xTt, r_dst, copy_eng, r_bank=None, store_q="pool",
                       do_cast=True):
            if r_bank is None:
                r_bank = r_ps[bank]
            if do_cast:
                t_cast(src_f32, r_src, xb, r_xb)
            pst = psb[bank].rearrange("p (c t) -> p c t", c=8)
            for c in range(8):
                kb.tr(pst[:, c, :], xb[:, c * 128:(c + 1) * 128], ident, [r_xb, r_c], [r_bank])
            kb.cp(copy_eng, xTt.rearrange("p (c t) -> p c t", c=8), pst, [r_bank], [r_xTt])
            kb.dma(store_q, dst_dram, xTt, [r_xTt], [r_dst])

        r_attn = [Res() for _ in range(NG)]
        if "A" in phases:
            xT = alloc(8 * T, BF16).rearrange("p (c t) -> p c t", c=8)
            r_xT = [Res() for _ in range(NT)]
            past = alloc(NT * NB, F32)
            own = alloc(NT * NB, F32)
            kb.dma("sp", past, c_past, wr=[r_c])
            kb.dma("sp", own, c_own, wr=[r_c])
            xbuf = [alloc(D, BF16) for _ in range(4)]
            r_xbuf = [Res() for _ in range(4)]
            for i in range(0 if cfg.get("skipA0") else NT):
                b = i % 2
                b4 = i % 4
                kb.dma("pool", xbuf[b4], x[i * 128:(i + 1) * 128, :], wr=[r_xbuf[b4]])
                pst = psb[b].rearrange("p (c t) -> p c t", c=8)
                for c in range(8):
                    kb.tr(pst[:, c, :], xbuf[b4][:, c * 128:(c + 1) * 128], ident, [r_xbuf[b4], r_c], [r_ps[b]])
                kb.cp("act" if i % 2 else "dve", xT[:, :, i * 128:(i + 1) * 128], pst, [r_ps[b]], [r_xT[i]])

            kb.cut(1)
            wh = [alloc(8 * 192, BF16).rearrange("p (c n) -> p c n", c=8) for _ in range(2)]
            r_wh = [Res(), Res()]
            kTb = [alloc(T, BF16) for _ in range(2)]
            qTb = [alloc(T, BF16) for _ in range(2)]
            r_k = [[Res() for _ in range(NG)] for _ in range(2)]
            r_q = [[Res() for _ in range(NG)] for _ in range(2)]
            Vb = [alloc(NT * 65, BF16).rearrange("p (i d) -> p i d", d=65) for _ in range(2)]
            NVB = (NT + 7) // 8
            r_v = [[Res() for _ in range(NVB)] for _ in range(2)]
            for b in range(2):
                kb.memset("pool", Vb[b][:, :, 64:65], 1.0, r_v[b])
            qf = alloc(T, F32)
            r_qf = [Res() for _ in range(NG)]
            km = alloc(NB, F32)
            r_km = Res()
            mbp = alloc(NT * 80, BF16).rearrange("p (i m) -> p i m", m=80)
            r_mbp = Res()
            kb.memset("pool", mbp, 0.0, [r_mbp])
            tmp = [alloc(NT * NB, F32) for _ in range(4)]
            r_tmp = [Res() for _ in range(4)]
            mx = [alloc(NT, F32) for _ in range(3)]
            r_mx = [Res() for _ in range(3)]
            Pt = [alloc(512, BF16) for _ in range(3)]
            r_pt = [Res() for _ in range(3)]
            rec = alloc(512, F32)
            r_rec = Res()
            otmp = alloc(512, F32)
            r_otmp = Res()
            on_ = [alloc(512, BF16) for _ in range(2)]
            r_on = [Res(), Res()]

            r_ser = Res()

            def v3(ap):
                return ap.rearrange("p (i n) -> p i n", n=NB)

            def prep(h):
                b = h % 2
                for sec in range(3):
                    kb.dma("pool", wh[b][:, :, sec * 64:(sec + 1) * 64],
                           w_qkv[:, sec * D + h * 64: sec * D + (h + 1) * 64].rearrange("(c p) n -> p c n", p=128),
                           wr=[r_wh[b]])
                kb.dma("pool", kTb[b][64:96, :], c_kaug[h], wr=r_k[b])
                kb.dma("pool", qTb[b][64:96, :], c_qaug[h], wr=r_q[b])
                kb.cut(21)
                yield
                for g in range(NG):
                    pb = g % 2 + cfg.get("kbank", 0)
                    pk = ps[pb][0:64, :]
                    for c in range(8):
                        kb.mm(pk, wh[b][:, c, 64:128], xT[:, c, g * 512:(g + 1) * 512], c == 0, c == 7,
                              [r_wh[b]] + r_xT[4 * g:4 * g + 4], [r_ps[pb]])
                    kb.red(km[0:64, 2 * g:2 * g + 2], pk.rearrange("p (a s) -> p a s", s=256), ALU.add,
                           [r_ps[pb]], [r_km, r_ser])
                    kb.cp("dve", kTb[b][0:64, g * 512:(g + 1) * 512], pk, [r_ps[pb], r_ser], [r_k[b][g]])
                    yield
                kb.cut(22)
                for g in range(NG):
                    pb = g % 2
                    pq = ps[pb][0:64, :]
                    for c in range(8):
                        kb.mm(pq, wh[b][:, c, 0:64], xT[:, c, g * 512:(g + 1) * 512], c == 0, c == 7,
                              [r_wh[b]] + r_xT[4 * g:4 * g + 4], [r_ps[pb]])
                    kb.ts("dve", qf[0:64, g * 512:(g + 1) * 512], pq, 0.125, None, ALU.mult, None,
                          [r_ps[pb]], [r_qf[g], r_ser])
                    kb.ts("dve", qTb[b][0:64, g * 512:(g + 1) * 512], pq, 0.125, None, ALU.mult, None,
                          [r_ps[pb], r_ser], [r_q[b][g]])
                    yield
                kb.cut(23)
                for i in range(NT):
                    pb = (i // 8) % 2
                    pv = ps[pb][:, (i % 8) * 64:(i % 8 + 1) * 64]
                    for c in range(8):
                        kb.mm(pv, xT[:, c, i * 128:(i + 1) * 128], wh[b][:, c, 128:192], c == 0, c == 7,
                              [r_wh[b], r_xT[i]], [r_ps[pb]])
                    if i % 8 == 7 or i == NT - 1:
                        n = i % 8 + 1
                        base = i - n + 1
                        kb.cp("dve", Vb[b][:, base:base + n, 0:64],
                              ps[pb][:, 0:n * 64].rearrange("p (i d) -> p i d", d=64), [r_ps[pb]], [r_v[b][i // 8]])
                        yield
                kb.cut(24)
                pg = ps[0][:, 0:NT * NB]
                for i in range(NT):
                    kb.mm(pg[:, i * NB:(i + 1) * NB], qf[0:64, i * 128:(i + 1) * 128], km[0:64, 0:NB], True, True,
                          [r_qf[i // 4], r_km], [r_ps[0]])
                yield
                kb.cut(25)
                g0, g1, g2, e1 = tmp
                kb.tt("dve", g0, pg, past, ALU.add, [r_ps[0], r_c], [r_tmp[0]])
                kb.red(mx[0], v3(g0), ALU.max, [r_tmp[0]], [r_mx[0]])
                kb.tt("dve", v3(e1), v3(g0), mx[0].unsqueeze(2).to_broadcast([128, NT, NB]), ALU.is_ge,
                      [r_tmp[0], r_mx[0]], [r_tmp[3]])
                kb.stt("dve", g1, e1, BIGNEG, g0, ALU.mult, ALU.add, [r_tmp[3], r_tmp[0]], [r_tmp[1]])
                kb.red(mx[1], v3(g1), ALU.max, [r_tmp[1]], [r_mx[1]])
                kb.tt("dve", v3(e1), v3(g1), mx[1].unsqueeze(2).to_broadcast([128, NT, NB]), ALU.is_ge,
                      [r_tmp[1], r_mx[1]], [r_tmp[3]])
                kb.stt("dve", g2, e1, BIGNEG, g1, ALU.mult, ALU.add, [r_tmp[3], r_tmp[1]], [r_tmp[2]])
                kb.red(mx[2], v3(g2), ALU.max, [r_tmp[2]], [r_mx[2]])
                kb.tt("dve", v3(e1), v3(g0), mx[2].unsqueeze(2).to_broadcast([128, NT, NB]), ALU.is_ge,
                      [r_tmp[0], r_mx[2]], [r_tmp[3]])
                kb.tt("dve", g1, e1, own, ALU.max, [r_tmp[3], r_c], [r_tmp[1]])
                kb.ts("dve", mbp[:, :, 64:64 + NB], v3(g1), -1.0, -NEG, ALU.add, ALU.mult, [r_tmp[1]], [r_mbp])
                yield
                kb.cut(26)
                for i in range(NT):
                    kb.mm(ps[1][0:80, (i % 4) * 128:(i % 4 + 1) * 128], mbp[:, i, :], ident, True, True,
                          [r_mbp, r_c], [r_ps[1]])
                    if i % 4 == 3:
                        kb.cp("dve", qTb[b][64:64 + NB, (i - 3) * 128:(i + 1) * 128], ps[1][64:64 + NB, 0:512],
                              [r_ps[1]], [r_q[b][i // 4]])
                        yield

            cnt = [0]

            def attend(h, nxt=None):
                b = h % 2
                dq = Deferred(2)
                dqn = Delayed()
                tcount = 0
                for g in range(NG):
                    ob = 5 + (g % 2)
                    OT = ps[ob]
                    nj = 4 * g + 4
                    for j in range(nj):
                        lo = 0 if j < 4 * g else 128 * (j - 4 * g)
                        k = cnt[0] % 3
                        cnt[0] += 1
                        sb_ = 2 + k
                        kb.mm(ps[sb_][:, lo:512], kTb[b][0:96, j * 128:(j + 1) * 128],
                              qTb[b][0:96, g * 512 + lo:(g + 1) * 512], True, j < 4 * g,
                              [r_k[b][j // 4], r_q[b][g]], [r_ps[sb_]])
                        if j >= 4 * g:
                            kb.mm(ps[sb_][:, lo:lo + 128], ident, tri[:, 0:128], False, True, [r_c], [r_ps[sb_]])
                        kb.act(Pt[k][:, lo:512], ps[sb_][:, lo:512], AF.Exp, [r_ps[sb_]], [r_pt[k]])

                        def pv(OT=OT, ob=ob, j=j, lo=lo, k=k, nj=nj):
                            kb.mm(OT[0:65, lo:512], Vb[b][:, j, 0:65], Pt[k][:, lo:512], j == 0, j == nj - 1,
                                  [r_v[b][j // 8], r_pt[k]], [r_ps[ob]])

                        dq.push(pv)
                        if j == nj - 1:
                            def norm1(OT=OT, ob=ob):
                                kb.act(rec[64:65, :], OT[64:65, :], AF.Ln, [r_ps[ob]], [r_rec])
                                kb.act(rec[64:65, :], rec[64:65, :], AF.Exp, [r_rec], [r_rec], scale=-1.0)
                                kb.cp("dve", otmp[0:64, :], OT[0:64, :], [r_ps[ob]], [r_otmp])

                            def norm2(g=g):
                                kb.mm(ps[7][0:64, :], onesf[64:65, 0:64], rec[64:65, :], True, True, [r_rec, r_c], [r_ps[7]])
                                k2 = g % 2
                                kb.tt("dve", on_[k2][0:64, :], otmp[0:64, :], ps[7][0:64, :], ALU.mult,
                                      [r_otmp, r_ps[7]], [r_on[k2]])
                                kb.dma("sp", attnT[h * 64:(h + 1) * 64, g * 512:(g + 1) * 512], on_[k2][0:64, :],
                                       [r_on[k2]], [r_attn[g]])

                            dq.push(norm1)
                            dqn.push(norm2, 4)
                        dq.step()
                        dqn.tick()
                        tcount += 1
                        if nxt is not None and tcount % 4 == 0:
                            next(nxt, None)
                dq.flush()
                dqn.flush()

            NHh = cfg.get("nheads", H)
            zf = [r for r in range(NSLOT * (S // 128))] if ("D" in phases and cfg.get("moe", "routed") == "routed") else []
            for _ in prep(0):
                pass
            for h in range(NHh):
                nxt = prep(h + 1) if h + 1 < NHh else None
                for _ in range(-(-len(zf) // max(1, NHh - h))):
                    r = zf.pop(0)
                    kb.dma("sp", xsorted[r * 128:(r + 1) * 128, :], zt, [r_zt], [r_xso])
                attend(h, nxt)
                if nxt is not None:
                    for _ in nxt:
                        pass
            kb.release()
            kb.cut(4)

        r_x1 = [Res() for _ in range(NT)]
        r_x1T = [Res() for _ in range(NT)]
        r_x2 = [Res() for _ in range(NT)]
        r_x2T = [Res() for _ in range(NT)]
        if "B" in phases:
            NFC = FF // 128
            Wg = alloc(8 * FF, BF16).rearrange("p (c n) -> p c n", c=8)
            Wu = alloc(8 * FF, BF16).rearrange("p (c n) -> p c n", c=8)
            r_wgu = Res()
            saved_persist = kb.persist
            kb.persist = kb.off
            Wo = alloc(8 * D, BF16).rearrange("p (c n) -> p c n", c=8)
            r_w = Res()
            kb.dma("pool", Wo, w_o_a.rearrange("(c p) n -> p c n", p=128), wr=[r_w])
            wgu_pieces = []
            for c in range(8):
                wgu_pieces.append((Wg[:, c, :], w_gate_d[c * 128:(c + 1) * 128, :]))
                wgu_pieces.append((Wu[:, c, :], w_up_d[c * 128:(c + 1) * 128, :]))
            g1b, b1b, r_gb1 = load_gb(0)
            aT = [alloc(8 * 512, BF16).rearrange("p (c t) -> p c t", c=8) for _ in range(2)]
            r_aT = [Res(), Res()]
            xt = [alloc(D, F32) for _ in range(2)]
            r_xt = [Res(), Res()]
            yb = [alloc(D, F32) for _ in range(2)]
            r_y = [Res(), Res()]
            x1 = [alloc(D, F32) for _ in range(2)]
            r_x1b = [Res(), Res()]
            xb16 = [alloc(D, BF16) for _ in range(2)]
            r_xb16 = [Res(), Res()]
            xTt = [alloc(D, BF16) for _ in range(2)]
            r_xTt = [Res(), Res()]
            lt = [kb.ln_tmp() for _ in range(2)]
            dqb = Deferred(1)
            for g in range(NG):
                gb = g % 2
                kb.dma("sp", aT[gb], attnT[:, g * 512:(g + 1) * 512].rearrange("(c p) t -> p c t", p=128),
                       [r_attn[g]], [r_aT[gb]])
                for t in range(4):
                    i = 4 * g + t
                    b = i % 2
                    npc = -(-len(wgu_pieces) // max(1, NT - 2))
                    for _ in range(npc if cfg.get("prefetch_wgu", False) else 0):
                        if wgu_pieces:
                            dst_, src_ = wgu_pieces.pop(0)
                            kb.dma("pool", dst_, src_, wr=[r_wgu])
                    for n in range(2):
                        for c in range(8):
                            kb.mm(ps[2 + n][:, :], aT[gb][:, c, t * 128:(t + 1) * 128], Wo[:, c, n * 512:(n + 1) * 512],
                                  c == 0, c == 7, [r_aT[gb], r_w], [r_ps[2 + n]])
                    kb.dma("sp", xt[b], x[i * 128:(i + 1) * 128, :], wr=[r_xt[b]])
                    for n in range(2):
                        kb.stt("dve", yb[b][:, n * 512:(n + 1) * 512], xt[b][:, n * 512:(n + 1) * 512], ALPHA,
                               ps[2 + n][:, :], ALU.mult, ALU.add, [r_xt[b], r_ps[2 + n]], [r_y[b]])
                    kb.layernorm(yb[b], r_y[b], g1b, b1b, r_gb1, x1[b], r_x1b[b], lt[b])
                    kb.dma("pool", x1s[i * 128:(i + 1) * 128, :], x1[b], [r_x1b[b]], [r_x1[i]])
                    t_cast(x1[b], r_x1b[b], xb16[b], r_xb16[b])

                    def tb(i=i, b=b):
                        to_T_block(x1[b], r_x1b[b], x1Ts[i], b, xb16[b], r_xb16[b], xTt[b], r_xTt[b], r_x1T[i],
                                   "act" if i % 2 else "dve", do_cast=False)

                    dqb.push(tb)
                    dqb.step()
            dqb.flush()
            kb.release()
            kb.cut(5)
            while wgu_pieces:
                dst_, src_ = wgu_pieces.pop(0)
                kb.dma("pool", dst_, src_, wr=[r_wgu])

            Wd = alloc(NFC * D, BF16).rearrange("p (c n) -> p c n", c=NFC)
            r_w = Res()
            kb.dma("pool", Wd, w_down_d.rearrange("(c p) n -> p c n", p=128), wr=[r_w])
            g2b, b2b, r_gb2 = load_gb(1)
            xg = [alloc(8 * 256, BF16).rearrange("p (c t) -> p c t", c=8) for _ in range(2)]
            r_xg = [Res(), Res()]
            sg = [alloc(256, F32) for _ in range(2)]
            r_sg = [Res(), Res()]
            hT = [alloc(256, BF16) for _ in range(2)]
            r_hT = [Res(), Res()]
            xt = [alloc(D, F32) for _ in range(2)]
            r_xt = [Res(), Res()]
            yb = [alloc(D, F32) for _ in range(2)]
            r_y = [Res(), Res()]
            x2 = [alloc(D, F32) for _ in range(2)]
            r_x2b = [Res(), Res()]
            xb16 = [alloc(D, BF16) for _ in range(2)]
            r_xb16 = [Res(), Res()]
            xTt = [alloc(D, BF16) for _ in range(2)]
            r_xTt = [Res(), Res()]
            lt = [kb.ln_tmp() for _ in range(2)]
            r_g = [Res(excl=True), Res(excl=True)]
            r_u = [Res(excl=True), Res(excl=True)]
            dq = Deferred(1)
            for tg in range(NT // 2):
                gb = tg % 2
                for t in range(2):
                    kb.dma("sp", xg[gb][:, :, t * 128:(t + 1) * 128],
                           x1Ts[2 * tg + t].rearrange("p (c t) -> p c t", c=8), [r_x1T[2 * tg + t]], [r_xg[gb]])
                for c in range(NFC):
                    cb = c % 2
                    pgu = ps[cb]
                    for k in range(8):
                        kb.mm(pgu[:, 0:256], Wg[:, k, c * 128:(c + 1) * 128], xg[gb][:, k, :], k == 0, k == 7,
                              [r_wgu, r_xg[gb]], [r_g[cb]])
                    for k in range(8):
                        kb.mm(ps[6 + cb][:, 0:256], Wu[:, k, c * 128:(c + 1) * 128], xg[gb][:, k, :], k == 0, k == 7,
                              [r_wgu, r_xg[gb]], [r_u[cb]])
                    kb.act(sg[cb], pgu[:, 0:256], AF.Silu, [r_g[cb]], [r_sg[cb]])
                    kb.tt("dve", hT[cb], sg[cb], ps[6 + cb][:, 0:256], ALU.mult, [r_sg[cb], r_u[cb]], [r_hT[cb]])

                    def down(c=c, cb=cb):
                        for t in range(2):
                            for n in range(2):
                                bank = 2 + 2 * t + n
                                kb.mm(ps[bank][:, :], hT[cb][:, t * 128:(t + 1) * 128], Wd[:, c, n * 512:(n + 1) * 512],
                                      c == 0, c == NFC - 1, [r_hT[cb], r_w], [r_ps[bank]])

                    dq.push(down)
                    dq.step()

                def fin(tg=tg):
                    for t in range(2):
                        i = 2 * tg + t
                        b = i % 2
                        kb.dma("sp", xt[b], x1s[i * 128:(i + 1) * 128, :], [r_x1[i]], [r_xt[b]])
                        for n in range(2):
                            bank = 2 + 2 * t + n
                            kb.stt("dve", yb[b][:, n * 512:(n + 1) * 512], xt[b][:, n * 512:(n + 1) * 512], ALPHA,
                                   ps[bank][:, :], ALU.mult, ALU.add, [r_xt[b], r_ps[bank]], [r_y[b]])
                        kb.layernorm(yb[b], r_y[b], g2b, b2b, r_gb2, x2[b], r_x2b[b], lt[b])
                        kb.dma("pool", x2s[i * 128:(i + 1) * 128, :], x2[b], [r_x2b[b]], [r_x2[i]])
                        t_cast(x2[b], r_x2b[b], xb16[b], r_xb16[b])

                def fin2(tg=tg):
                    for t in range(2):
                        i = 2 * tg + t
                        b = i % 2
                        to_T_block(x2[b], r_x2b[b], x2Ts[i], b, xb16[b], r_xb16[b], xTt[b], r_xTt[b], r_x2T[i],
                                   "act" if i % 2 else "dve", r_bank=r_g[b], do_cast=False)

                dq.push(fin)
                dq.push(fin2)
            dq.flush()
            kb.persist = saved_persist
            kb.release()

        r_x3 = [Res() for _ in range(NT)]
        r_x3T = [Res() for _ in range(NT)]
        if "C" in phases:
            Wq2 = alloc(8 * D, BF16).rearrange("p (c n) -> p c n", c=8)
            Wkv = alloc(8 * 256, BF16).rearrange("p (c n) -> p c n", c=8)
            Wo2 = alloc(H * D, BF16).rearrange("p (h n) -> p h n", h=H)
            wr = alloc(8 * NE, F32).rearrange("p (c e) -> p c e", c=8)
            r_w = Res()
            kb.dma("pool", Wq2, w_q_b.rearrange("(c p) n -> p c n", p=128), wr=[r_w])
            kb.dma("pool", Wkv, w_kv.rearrange("(c p) n -> p c n", p=128), wr=[r_w])
            kb.dma("pool", Wo2[0:64, :, :], w_o_b.rearrange("(h p) n -> p h n", p=64), wr=[r_w])
            kb.dma("sp", wr, w_router.rearrange("(c p) e -> p c e", p=128), wr=[r_w])
            g3b, b3b, r_gb3 = load_gb(2)
            sk = alloc(H, F32)
            esk = alloc(H * 128, F32)
            ones64 = alloc(64, BF16)
            r_sk = Res()
            kb.dma("sp", sk[0:64, :], sinks.to_broadcast([64, H]), wr=[r_sk])
            kb.act(sk[0:64, :], sk[0:64, :], AF.Exp, [r_sk], [r_sk])
            kb.cp("dve", esk[0:64, :].rearrange("p (h t) -> p h t", t=128),
                  sk[0:64, :].unsqueeze(2).to_broadcast([64, H, 128]), [r_sk], [r_sk])
            kb.memset("pool", ones64, 1.0, [r_sk])
            kcur = alloc(2 * 8 * 128, BF16).rearrange("p (k s t) -> p k s t", k=2, s=8)
            kprev = alloc(2 * 8 * 128, BF16).rearrange("p (k s t) -> p k s t", k=2, s=8)
            r_kc = [Res(), Res()]
            V2 = alloc(8 * 2 * 65, BF16).rearrange("p (s k d) -> p s k d", s=8, k=2)
            r_v2 = [Res(), Res()]
            kb.memset("pool", V2[:, :, :, 64:65], 1.0, r_v2)
            for kv in range(2):
                for s in range(8):
                    kb.dma("pool", kcur[64:96, kv, s, :], c_kaug2[0], wr=r_kc)
                    kb.dma("pool", kprev[64:96, kv, s, :], c_kaug2[1], wr=r_kc)
            q2 = alloc(4 * H * 128, BF16).rearrange("p (t h s) -> p t h s", t=4, h=H)
            r_q2 = Res()
            for t in range(4):
                kb.dma("pool", q2[64:96, t, :, :], c_qaug2.rearrange("h r s -> r h s"), wr=[r_q2])
            on2 = alloc(4 * H * 128, BF16).rearrange("p (t h s) -> p t h s", t=4, h=H)
            r_on2 = [Res() for _ in range(4)]
            xg = [alloc(8 * 512, BF16).rearrange("p (c t) -> p c t", c=8) for _ in range(2)]
            r_xg = [Res(), Res()]
            Pc = [alloc(512, BF16) for _ in range(2)]
            Pp = [alloc(512, BF16) for _ in range(2)]
            r_pc = [Res(), Res()]
            r_pp = [Res(), Res()]
            den = alloc(512, F32)
            rec = alloc(512, F32)
            r_den = Res()
            r_rec = Res()
            otmp = alloc(512, F32)
            r_otmp = Res()
            xt = [alloc(D, F32) for _ in range(2)]
            r_xt = [Res(), Res()]
            yb = [alloc(D, F32) for _ in range(2)]
            r_y = [Res(), Res()]
            x3 = [alloc(D, F32) for _ in range(3)]
            r_x3b = [Res() for _ in range(3)]
            xb16 = [alloc(D, BF16) for _ in range(3)]
            r_xb16 = [Res() for _ in range(3)]
            xTt = [alloc(D, BF16) for _ in range(3)]
            r_xTt = [Res() for _ in range(3)]
            x3Tf = alloc(D, F32).rearrange("p (c t) -> p c t", c=8)
            r_x3Tf = Res()
            lg = [alloc(NE, F32) for _ in range(4)]
            r_lg = [Res() for _ in range(4)]
            sm = [alloc(1, F32) for _ in range(6)]
            r_sm = [Res() for _ in range(6)]
            lt = [kb.ln_tmp() for _ in range(2)]
            cc = 0
            dq = Deferred(1)
            for g in range(NG):
                gb = g % 2
                so = (g % 2) * 4
                for t in range(4):
                    kb.dma("sp", xg[gb][:, :, t * 128:(t + 1) * 128],
                           x2Ts[4 * g + t].rearrange("p (c t) -> p c t", c=8), [r_x2T[4 * g + t]], [r_xg[gb]])
                for kv in range(2):
                    pk = ps[kv][0:64, :]
                    for c in range(8):
                        kb.mm(pk, Wkv[:, c, kv * 64:(kv + 1) * 64], xg[gb][:, c, :], c == 0, c == 7,
                              [r_w, r_xg[gb]], [r_ps[kv]])
                    pk3 = pk.rearrange("p (s t) -> p s t", t=128)
                    kb.cp("act", kcur[0:64, kv, so:so + 4, :], pk3, [r_ps[kv]], [r_kc[gb]])
                    kb.cp("dve", kprev[0:64, kv, so:so + 4, :], pk3, [r_ps[kv]], [r_kc[gb]])
                for t in range(4):
                    for c in range(8):
                        kb.mm(ps[2][:, t * 128:(t + 1) * 128], xg[gb][:, c, t * 128:(t + 1) * 128], Wkv[:, c, 128:256],
                              c == 0, c == 7, [r_w, r_xg[gb]], [r_ps[2]])
                kb.cp("dve", V2[:, so:so + 4, :, 0:64], ps[2][:, :].rearrange("p (s k d) -> p s k d", s=4, k=2),
                      [r_ps[2]], [r_v2[gb]])
                for h in range(H):
                    pb = h % 2
                    pq = ps[pb][0:64, :]
                    for c in range(8):
                        kb.mm(pq, Wq2[:, c, h * 64:(h + 1) * 64], xg[gb][:, c, :], c == 0, c == 7,
                              [r_w, r_xg[gb]], [r_ps[pb]])
                    pq3 = pq.rearrange("p (t s) -> p t s", s=128)
                    if h % 2:
                        kb.ts("dve", q2[0:64, :, h, :], pq3, 0.125, None, ALU.mult, None, [r_ps[pb]], [r_q2])
                    else:
                        kb.act(q2[0:64, :, h, :], pq3, AF.Copy, [r_ps[pb]], [r_q2], scale=0.125)
                for t in range(4):
                    i = 4 * g + t
                    sl = so + t
                    psl = (sl - 1) % 8
                    rgp = (gb if t > 0 else 1 - gb)
                    for kv in range(2):
                        for half in range(2):
                            hs = kv * 8 + half * 4
                            rhs = q2[0:96, t, hs:hs + 4, :]
                            k = cc % 2
                            cc += 1
                            bc_, bp_ = 2 + k, 4 + k
                            kb.mm(ps[bc_][:, :], kcur[0:96, kv, sl, :], rhs, True, False, [r_kc[gb], r_q2], [r_ps[bc_]])
                            kb.mm(ps[bc_][:, :], ident, tri, False, True, [r_c], [r_ps[bc_]])
                            kb.act(Pc[k], ps[bc_][:, :], AF.Exp, [r_ps[bc_]], [r_pc[k]])
                            if i > 0:
                                kb.mm(ps[bp_][:, :], kprev[0:96, kv, psl, :], rhs, True, False, [r_kc[rgp], r_q2], [r_ps[bp_]])
                                kb.mm(ps[bp_][:, :], ident, tri2, False, True, [r_c], [r_ps[bp_]])
                                kb.act(Pp[k], ps[bp_][:, :], AF.Exp, [r_ps[bp_]], [r_pp[k]])

                            def pv(i=i, t=t, kv=kv, hs=hs, k=k, sl=sl, psl=psl, rgp=rgp, gb=gb):
                                kb.mm(ps[6][0:64, :], V2[:, sl, kv, 0:64], Pc[k], True, i == 0, [r_v2[gb], r_pc[k]], [r_ps[6]])
                                if i > 0:
                                    kb.mm(ps[6][0:64, :], V2[:, psl, kv, 0:64], Pp[k], False, True, [r_v2[rgp], r_pp[k]], [r_ps[6]])
                                kb.mm(ps[7][0:64, :], ones64[:, 0:64], Pc[k], True, i == 0, [r_sk, r_pc[k]], [r_ps[7]])
                                if i > 0:
                                    kb.mm(ps[7][0:64, :], ones64[:, 0:64], Pp[k], False, True, [r_sk, r_pp[k]], [r_ps[7]])
                                kb.tt("dve", den[0:64, :], ps[7][0:64, :], esk[0:64, hs * 128:(hs + 4) * 128], ALU.add,
                                      [r_ps[7], r_sk], [r_den])
                                kb.act(den[0:64, :], den[0:64, :], AF.Ln, [r_den], [r_den])
                                kb.act(rec[0:64, :], den[0:64, :], AF.Exp, [r_den], [r_rec], scale=-1.0)
                                kb.tt("dve", on2[0:64, t, hs:hs + 4, :], ps[6][0:64, :].rearrange("p (h s) -> p h s", s=128),
                                      rec[0:64, :].rearrange("p (h s) -> p h s", s=128), ALU.mult,
                                      [r_ps[6], r_rec], [r_on2[t]])

                            dq.push(pv)
                            dq.step()

                    def tilefin(i=i, t=t):
                        b = i % 3
                        b2 = i % 2
                        for n in range(2):
                            for h in range(H):
                                kb.mm(ps[n][:, :], on2[0:64, t, h, :], Wo2[0:64, h, n * 512:(n + 1) * 512], h == 0, h == H - 1,
                                      [r_on2[t], r_w], [r_ps[n]])
                        kb.dma("sp", xt[b2], x2s[i * 128:(i + 1) * 128, :], [r_x2[i]], [r_xt[b2]])
                        for n in range(2):
                            kb.stt("dve", yb[b2][:, n * 512:(n + 1) * 512], xt[b2][:, n * 512:(n + 1) * 512], ALPHA,
                                   ps[n][:, :], ALU.mult, ALU.add, [r_xt[b2], r_ps[n]], [r_y[b2]])
                        kb.layernorm(yb[b2], r_y[b2], g3b, b3b, r_gb3, x3[b], r_x3b[b], lt[b2])
                        kb.dma("pool", x3s[i * 128:(i + 1) * 128, :], x3[b], [r_x3b[b]], [r_x3[i]])
                        t_cast(x3[b], r_x3b[b], xb16[b], r_xb16[b])
                        kb.dma("pool", x3bs[i * 128:(i + 1) * 128, :], xb16[b], [r_xb16[b]], [r_x3[i]])

                    def tilefin2(i=i, t=t):
                        b = i % 3
                        to_T_block(x3[b], r_x3b[b], x3Ts[i], 0, xb16[b], r_xb16[b], xTt[b], r_xTt[b], r_x3T[i], "dve",
                                   do_cast=False)
                        for half in range(2):
                            pT = ps[half][:, :].rearrange("p (c t) -> p c t", c=4)
                            for c in range(4):
                                cc_ = half * 4 + c
                                kb.tr(pT[:, c, :], x3[b][:, cc_ * 128:(cc_ + 1) * 128], identf, [r_x3b[b], r_c], [r_ps[half]])
                            kb.cp("dve" if half else "act", x3Tf[:, half * 4:(half + 1) * 4, :], pT, [r_ps[half]], [r_x3Tf])
                        for c in range(8):
                            kb.mm(ps[1][:, 0:NE], x3Tf[:, c, :], wr[:, c, :], c == 0, c == 7, [r_x3Tf, r_w], [r_ps[1]])
                        l0, l1, e1_, e2_ = lg
                        m1, m2, dl, ex, w1, w2 = sm
                        E1i = E1s[:, i * NE:(i + 1) * NE]
                        E2i = E2s[:, i * NE:(i + 1) * NE]
                        kb.cp("dve", l0, ps[1][:, 0:NE], [r_ps[1]], [r_lg[0]])
                        kb.red(m1, l0, ALU.max, [r_lg[0]], [r_sm[0]])
                        kb.ts("dve", E1i, l0, m1[:, 0:1], None, ALU.is_ge, None, [r_lg[0], r_sm[0]], [r_lg[2], r_route])
                        kb.stt("dve", l1, E1i, BIGNEG, l0, ALU.mult, ALU.add, [r_lg[2], r_lg[0]], [r_lg[1]])
                        kb.red(m2, l1, ALU.max, [r_lg[1]], [r_sm[1]])
                        kb.ts("dve", E2i, l1, m2[:, 0:1], None, ALU.is_ge, None, [r_lg[1], r_sm[1]], [r_lg[3], r_route])
                        kb.tt("dve", dl, m2, m1, ALU.subtract, [r_sm[0], r_sm[1]], [r_sm[2]])
                        kb.act(ex, dl, AF.Exp, [r_sm[2]], [r_sm[3]])
                        kb.ts("dve", w1, ex, 1.0, None, ALU.add, None, [r_sm[3]], [r_sm[4]])
                        kb.recip(W1s[:, i:i + 1], w1, [r_sm[4]], [r_sm[5], r_route])
                        kb.tt("dve", W2s[:, i:i + 1], ex, W1s[:, i:i + 1], ALU.mult, [r_sm[3], r_sm[5]], [r_sm[5], r_route])
                        kb.ts("dve", e1_, E1i, W1s[:, i:i + 1], None, ALU.mult, None, [r_lg[2], r_sm[5]], [r_lg[2]])
                        kb.stt("dve", gates[:, i * NE:(i + 1) * NE], E2i, W2s[:, i:i + 1], e1_, ALU.mult, ALU.add,
                               [r_lg[3], r_sm[5], r_lg[2]], [r_gates])

                    dq.push(tilefin)
                    dq.push(tilefin2)
            dq.flush()
            if dbg:
                kb.dma("sp", gates_d, gates, [r_gates], [])
            kb.release()

        if "D" in phases and cfg.get("moe", "routed") == "dense":
            NSC = FE // 512
            NHALF = 2 if NT >= 4 else 1
            HT = NT // NHALF
            g4b, b4b, r_gb4 = load_gb(3)
            acc = alloc(HT * D, F32).rearrange("p (i n) -> p i n", n=D)
            r_acc = [Res() for _ in range(HT)]
            x3T = alloc(8 * HT * 128, BF16).rearrange("p (c t) -> p c t", c=8)
            r_x3Tr = [Res() for _ in range(HT)]
            wg = [alloc(8 * 512, BF16).rearrange("p (c n) -> p c n", c=8) for _ in range(2)]
            wu = [alloc(8 * 512, BF16).rearrange("p (c n) -> p c n", c=8) for _ in range(2)]
            wd = [alloc(4 * D, BF16).rearrange("p (c n) -> p c n", c=4) for _ in range(2)]
            r_wb = [Res(), Res()]
            sg = [alloc(256, F32) for _ in range(2)]
            r_sg = [Res(), Res()]
            hT = [alloc(256, BF16) for _ in range(2)]
            r_hT = [Res(), Res()]
            xt = [alloc(D, F32) for _ in range(2)]
            r_xt = [Res(), Res()]
            yb = [alloc(D, F32) for _ in range(2)]
            r_y = [Res(), Res()]
            xo = [alloc(D, F32) for _ in range(2)]
            r_xo = [Res(), Res()]
            lt = [kb.ln_tmp() for _ in range(2)]
            wcnt = 0
            ccnt = 0
            r_g = [Res(excl=True), Res(excl=True)]
            r_u = [Res(excl=True), Res(excl=True)]
            dq = Deferred(1)
            for hf in range(NHALF):
                t0 = hf * HT
                for it in range(HT):
                    kb.dma("sp", x3T[:, :, it * 128:(it + 1) * 128], x3Ts[t0 + it].rearrange("p (c t) -> p c t", c=8),
                           [r_x3T[t0 + it]], [r_x3Tr[it]])
                    kb.memset("pool", acc[:, it, :], 0.0, [r_acc[it]])
                for e in range(NE):
                    for sc in range(NSC):
                        wb = wcnt % 2
                        wcnt += 1
                        kb.dma("pool", wg[wb], w_gate_e[e, :, sc * 512:(sc + 1) * 512].rearrange("(c p) n -> p c n", p=128),
                               wr=[r_wb[wb]])
                        kb.dma("pool", wu[wb], w_up_e[e, :, sc * 512:(sc + 1) * 512].rearrange("(c p) n -> p c n", p=128),
                               wr=[r_wb[wb]])
                        kb.dma("pool", wd[wb], w_down_e[e, sc * 512:(sc + 1) * 512, :].rearrange("(c p) n -> p c n", p=128),
                               wr=[r_wb[wb]])
                        for tg in range(HT // 2):
                            xs = x3T[:, :, tg * 256:(tg + 1) * 256]
                            rx = [r_x3Tr[2 * tg], r_x3Tr[2 * tg + 1]]
                            for c in range(4):
                                cb = ccnt % 2
                                ccnt += 1
                                pgu = ps[cb]
                                for k in range(8):
                                    kb.mm(pgu[:, 0:256], wg[wb][:, k, c * 128:(c + 1) * 128], xs[:, k, :], k == 0, k == 7,
                                          [r_wb[wb]] + rx, [r_g[cb]])
                                for k in range(8):
                                    kb.mm(ps[6 + cb][:, 0:256], wu[wb][:, k, c * 128:(c + 1) * 128], xs[:, k, :], k == 0, k == 7,
                                          [r_wb[wb]] + rx, [r_u[cb]])
                                kb.act(sg[cb], pgu[:, 0:256], AF.Silu, [r_g[cb]], [r_sg[cb]])
                                kb.tt("dve", hT[cb], sg[cb], ps[6 + cb][:, 0:256], ALU.mult, [r_sg[cb], r_u[cb]], [r_hT[cb]])

                                def down(c=c, cb=cb, wb=wb):
                                    for t in range(2):
                                        for n in range(2):
                                            bank = 2 + 2 * t + n
                                            kb.mm(ps[bank][:, :], hT[cb][:, t * 128:(t + 1) * 128],
                                                  wd[wb][:, c, n * 512:(n + 1) * 512],
                                                  c == 0, c == 3, [r_hT[cb], r_wb[wb]], [r_ps[bank]])

                                dq.push(down)
                                dq.step()

                            def accupd(tg=tg, e=e):
                                for t in range(2):
                                    it = 2 * tg + t
                                    gi = t0 + it
                                    for n in range(2):
                                        bank = 2 + 2 * t + n
                                        kb.stt("dve", acc[:, it, n * 512:(n + 1) * 512], ps[bank][:, :],
                                               gates[:, gi * NE + e:gi * NE + e + 1], acc[:, it, n * 512:(n + 1) * 512],
                                               ALU.mult, ALU.add, [r_ps[bank], r_gates, r_acc[it]], [r_acc[it]])

                            dq.push(accupd)
                dq.flush()
                for it in range(HT):
                    gi = t0 + it
                    b = it % 2
                    kb.dma("sp", xt[b], x3s[gi * 128:(gi + 1) * 128, :], [r_x3[gi]], [r_xt[b]])
                    kb.stt("dve", yb[b], xt[b], ALPHA, acc[:, it, :], ALU.mult, ALU.add, [r_xt[b], r_acc[it]], [r_y[b]])
                    kb.layernorm(yb[b], r_y[b], g4b, b4b, r_gb4, xo[b], r_xo[b], lt[b])
                    kb.dma("sp", out[gi * 128:(gi + 1) * 128, :], xo[b], [r_xo[b]], [])
        if "D" in phases and cfg.get("moe", "routed") == "routed":
            NSC = NSCR
            TS = S // 128
            I32 = mybir.dt.int32
            wg_flat = w_gate_e.rearrange("e d (n c) -> (e d n) c", c=512)
            wu_flat = w_up_e.rearrange("e d (n c) -> (e d n) c", c=512)
            wd_flat = w_down_e.rearrange("e f d -> (e f) d")
            g4b, b4b, r_gb4 = load_gb(3)
            lst = alloc(128, F32)
            onesm = alloc(128, F32)
            thr = alloc(NE * KMAX, F32)
            sidx = alloc(NSLOT * NE, F32)
            gp = alloc(8 * NSC, F32)
            dp = alloc(NSC * 4, F32)
            r_k = Res()
            kb.dma("sp", lst, c_lstrict, wr=[r_k])
            kb.dma("sp", onesm, c_ones, wr=[r_k])
            kb.dma("sp", thr, c_thr, wr=[r_k])
            kb.dma("sp", sidx, c_sidx, wr=[r_k])
            kb.dma("sp", gp, c_gp, wr=[r_k])
            kb.dma("sp", dp, c_dp, wr=[r_k])
            Mm = alloc(NT * NE, F32)
            Msum = alloc((NT + 1) * NE, F32)
            Rk = alloc(NT * NE, F32)
            ntot = alloc(NE, F32)
            cmp1 = alloc(NE * KMAX, F32)
            ns = alloc(NE, F32)
            cs = alloc(NE, F32)
            cend = alloc(NE, F32)
            offs = alloc(NE, F32)
            tmp3 = alloc(NT * NE, F32)
            tmp4 = alloc(NT * NE, F32)
            pos1f = alloc(NT, F32)
            pos2f = alloc(NT, F32)
            pos1i = alloc(NT, I32 if False else F32).bitcast(I32)
            pos2i = alloc(NT, F32).bitcast(I32)
            cmp2 = alloc(NSLOT * NE, F32)
            esf = alloc(NSLOT, F32)
            A3 = alloc(NSLOT * 8 * NSC, F32)
            idxG = alloc(NSLOT * 8 * NSC, F32).bitcast(I32)
            B3 = alloc(NSLOT * NSC * 4, F32)
            idxD = alloc(NSLOT * NSC * 4, F32).bitcast(I32)
            r_t = [Res() for _ in range(16)]
            r_idx = Res()
            kb.tt("dve", Mm, E1s, E2s, ALU.add, [r_route], [r_t[0]])
            kb.memset("dve", Msum[:, 0:NE], 0.0, [r_t[1]])
            for i in range(NT):
                kb.tt("dve", Msum[:, (i + 1) * NE:(i + 2) * NE], Msum[:, i * NE:(i + 1) * NE], Mm[:, i * NE:(i + 1) * NE],
                      ALU.add, [r_t[0], r_t[1]], [r_t[1]])
            for i in range(NT):
                kb.mm(ps[0][:, i * NE:(i + 1) * NE], lst, Mm[:, i * NE:(i + 1) * NE], True, False, [r_k, r_t[0]], [r_ps[0]])
                kb.mm(ps[0][:, i * NE:(i + 1) * NE], onesm, Msum[:, i * NE:(i + 1) * NE], False, True, [r_k, r_t[1]], [r_ps[0]])
            kb.mm(ps[1][:, 0:NE], onesm, Msum[:, NT * NE:(NT + 1) * NE], True, True, [r_k, r_t[1]], [r_ps[1]])
            kb.cp("dve", Rk, ps[0][:, 0:NT * NE], [r_ps[0]], [r_t[2]])
            kb.cp("dve", ntot, ps[1][:, 0:NE], [r_ps[1]], [r_t[3]])
            kb.tt("dve", cmp1.rearrange("p (e k) -> p e k", k=KMAX), ntot.unsqueeze(2).to_broadcast([128, NE, KMAX]),
                  thr.rearrange("p (e k) -> p e k", k=KMAX), ALU.is_gt, [r_t[3], r_k], [r_t[4]])
            kb.red(ns, cmp1.rearrange("p (e k) -> p e k", k=KMAX), ALU.add, [r_t[4]], [r_t[5]])
            kb.memset("dve", cs[:, 0:1], 0.0, [r_t[6]])
            for e in range(1, NE):
                kb.tt("dve", cs[:, e:e + 1], cs[:, e - 1:e], ns[:, e - 1:e], ALU.add, [r_t[5], r_t[6]], [r_t[6]])
            kb.tt("dve", cend, cs, ns, ALU.add, [r_t[5], r_t[6]], [r_t[7]])
            kb.ts("dve", offs, cs, float(S), None, ALU.mult, None, [r_t[6]], [r_t[8]])
            kb.tt("dve", tmp3.rearrange("p (i e) -> p i e", e=NE), Rk.rearrange("p (i e) -> p i e", e=NE),
                  offs.unsqueeze(1).to_broadcast([128, NT, NE]), ALU.add, [r_t[2], r_t[8]], [r_t[9]])
            kb.tt("dve", tmp4, tmp3, E1s, ALU.mult, [r_t[9], r_route], [r_t[10]])
            kb.red(pos1f, tmp4.rearrange("p (i e) -> p i e", e=NE), ALU.add, [r_t[10]], [r_t[11]])
            kb.cp("dve", pos1i, pos1f, [r_t[11]], [r_idx])
            kb.tt("dve", tmp4, tmp3, E2s, ALU.mult, [r_t[9], r_route], [r_t[10]])
            kb.red(pos2f, tmp4.rearrange("p (i e) -> p i e", e=NE), ALU.add, [r_t[10]], [r_t[11]])
            kb.cp("dve", pos2i, pos2f, [r_t[11]], [r_idx])
            kb.tt("dve", cmp2.rearrange("p (s e) -> p s e", e=NE), cend.unsqueeze(1).to_broadcast([128, NSLOT, NE]),
                  sidx.rearrange("p (s e) -> p s e", e=NE), ALU.is_le, [r_t[7], r_k], [r_t[12]])
            kb.red(esf, cmp2.rearrange("p (s e) -> p s e", e=NE), ALU.add, [r_t[12]], [r_t[13]])
            kb.ts("dve", esf, esf, float(NE - 1), None, ALU.min, None, [r_t[13]], [r_t[13]])
            kb.cp("dve", A3.rearrange("p (s j) -> p s j", s=NSLOT), esf.unsqueeze(2).to_broadcast([128, NSLOT, 8 * NSC]),
                  [r_t[13]], [r_t[14]])
            kb.stt("dve", A3.rearrange("p (s j) -> p s j", s=NSLOT), A3.rearrange("p (s j) -> p s j", s=NSLOT), float(D * NSC),
                   gp.unsqueeze(1).to_broadcast([128, NSLOT, 8 * NSC]), ALU.mult, ALU.add, [r_t[14], r_k], [r_t[14]])
            kb.cp("dve", idxG, A3, [r_t[14]], [r_idx])
            kb.cp("dve", B3.rearrange("p (s j) -> p s j", s=NSLOT), esf.unsqueeze(2).to_broadcast([128, NSLOT, NSC * 4]),
                  [r_t[13]], [r_t[15]])
            kb.stt("dve", B3.rearrange("p (s j) -> p s j", s=NSLOT), B3.rearrange("p (s j) -> p s j", s=NSLOT), float(FE),
                   dp.unsqueeze(1).to_broadcast([128, NSLOT, NSC * 4]), ALU.mult, ALU.add, [r_t[15], r_k], [r_t[15]])
            kb.cp("dve", idxD, B3, [r_t[15]], [r_idx])
            xb = [alloc(D, BF16) for _ in range(4)]
            r_xb = [Res() for _ in range(4)]
            for i in range(NT):
                b = i % 4
                kb.dma("sp", xb[b], x3bs[i * 128:(i + 1) * 128, :], [r_x3[i]], [r_xb[b]])
                kb.scatter(xsorted[:, :], pos1i[:, i:i + 1], xb[b], [r_xb[b], r_idx], [r_xso])
                kb.scatter(xsorted[:, :], pos2i[:, i:i + 1], xb[b], [r_xb[b], r_idx], [r_xso])
            P.barrier()
            d2_mark = kb.off
            xsl = [alloc(D, BF16) for _ in range(2)]
            r_xsl = [Res(), Res()]
            xsT = alloc(8 * S, BF16).rearrange("p (c t) -> p c t", c=8)
            r_xsT = [Res() for _ in range(TS)]
            acc = alloc(TS * D, F32).rearrange("p (i n) -> p i n", n=D)
            r_acc = [Res() for _ in range(TS)]
            wg = [alloc(8 * 512, BF16).rearrange("p (c n) -> p c n", c=8) for _ in range(3)]
            wu = [alloc(8 * 512, BF16).rearrange("p (c n) -> p c n", c=8) for _ in range(3)]
            wd = [alloc(4 * D, BF16).rearrange("p (c n) -> p c n", c=4) for _ in range(3)]
            r_wb = [Res(), Res(), Res()]
            sg = [alloc(256, F32) for _ in range(2)]
            r_sg = [Res(), Res()]
            hT = [alloc(256, BF16) for _ in range(2)]
            r_hT = [Res(), Res()]
            r_g = [Res(excl=True), Res(excl=True)]
            r_u = [Res(excl=True), Res(excl=True)]
            r_ys = Res()
            dq = Deferred(1)
            wcnt = 0
            ccnt = 0
            for sl in range(NSLOT):
                for t in range(TS):
                    b = t % 2
                    kb.dma("sp", xsl[b], xsorted[(sl * TS + t) * 128:(sl * TS + t + 1) * 128, :], [r_xso], [r_xsl[b]])
                    pst = psb[b].rearrange("p (c t) -> p c t", c=8)
                    for c in range(8):
                        kb.tr(pst[:, c, :], xsl[b][:, c * 128:(c + 1) * 128], ident, [r_xsl[b], r_c], [r_g[b]])
                    kb.cp("act" if t % 2 else "dve", xsT[:, :, t * 128:(t + 1) * 128], pst, [r_g[b]], [r_xsT[t]])
                for sc in range(NSC):
                    wb = wcnt % 3
                    wcnt += 1
                    for c in range(8):
                        col = (sl * 8 + c) * NSC + sc
                        kb.gather(wg[wb][:, c, :], wg_flat, idxG[:, col:col + 1], [r_idx], [r_wb[wb]])
                        kb.gather(wu[wb][:, c, :], wu_flat, idxG[:, col:col + 1], [r_idx], [r_wb[wb]])
                    for c in range(4):
                        col = (sl * NSC + sc) * 4 + c
                        kb.gather(wd[wb][:, c, :], wd_flat, idxD[:, col:col + 1], [r_idx], [r_wb[wb]])
                    for tg in range(TS // 2):
                        xs_ = xsT[:, :, tg * 256:(tg + 1) * 256]
                        rx = [r_xsT[2 * tg], r_xsT[2 * tg + 1]]
                        for c in range(4):
                            cb = ccnt % 2
                            ccnt += 1
                            for k in range(8):
                                kb.mm(ps[cb][:, 0:256], wg[wb][:, k, c * 128:(c + 1) * 128], xs_[:, k, :], k == 0, k == 7,
                                      [r_wb[wb]] + rx, [r_g[cb]])
                            for k in range(8):
                                kb.mm(ps[6 + cb][:, 0:256], wu[wb][:, k, c * 128:(c + 1) * 128], xs_[:, k, :], k == 0, k == 7,
                                      [r_wb[wb]] + rx, [r_u[cb]])
                            kb.act(sg[cb], ps[cb][:, 0:256], AF.Silu, [r_g[cb]], [r_sg[cb]])
                            kb.tt("dve", hT[cb], sg[cb], ps[6 + cb][:, 0:256], ALU.mult, [r_sg[cb], r_u[cb]], [r_hT[cb]])

                            def down(c=c, cb=cb, wb=wb):
                                for t in range(2):
                                    for n in range(2):
                                        bank = 2 + 2 * t + n
                                        kb.mm(ps[bank][:, :], hT[cb][:, t * 128:(t + 1) * 128],
                                              wd[wb][:, c, n * 512:(n + 1) * 512],
                                              c == 0, c == 3, [r_hT[cb], r_wb[wb]], [r_ps[bank]])

                            dq.push(down)
                            dq.step()

                        def accupd(tg=tg, sc=sc, sl=sl):
                            for t in range(2):
                                it = 2 * tg + t
                                for n in range(2):
                                    bank = 2 + 2 * t + n
                                    if sc == 0:
                                        kb.cp("dve", acc[:, it, n * 512:(n + 1) * 512], ps[bank][:, :],
                                              [r_ps[bank]], [r_acc[it]])
                                    else:
                                        kb.tt("dve", acc[:, it, n * 512:(n + 1) * 512], ps[bank][:, :],
                                              acc[:, it, n * 512:(n + 1) * 512], ALU.add, [r_ps[bank], r_acc[it]], [r_acc[it]])
                                if sc == NSC - 1:
                                    row = (sl * TS + it) * 128
                                    kb.dma("act", ysd[row:row + 128, :], acc[:, it, :], [r_acc[it]], [r_ys])

                        dq.push(accupd)
            dq.flush()
            P.barrier()
            kb.off = d2_mark
            r1 = [alloc(D, F32) for _ in range(3)]
            r2 = [alloc(D, F32) for _ in range(3)]
            r_r1 = [Res() for _ in range(3)]
            r_r2 = [Res() for _ in range(3)]
            xt = [alloc(D, F32) for _ in range(3)]
            r_xt = [Res() for _ in range(3)]
            yb = [alloc(D, F32) for _ in range(3)]
            r_y = [Res() for _ in range(3)]
            xo = [alloc(D, F32) for _ in range(3)]
            r_xo = [Res() for _ in range(3)]
            lt = [kb.ln_tmp() for _ in range(3)]
            for i in range(NT):
                b = i % 3
                b3 = i % 3
                kb.gather(r1[b3], ysd[:, :], pos1i[:, i:i + 1], [r_idx, r_ys], [r_r1[b3]])
                kb.gather(r2[b3], ysd[:, :], pos2i[:, i:i + 1], [r_idx, r_ys], [r_r2[b3]])
                kb.dma("sp", xt[b], x3s[i * 128:(i + 1) * 128, :], [r_x3[i]], [r_xt[b]])
                kb.act(r1[b3], r1[b3], AF.Copy, [r_r1[b3], r_route], [r_r1[b3]], scale=W1s[:, i:i + 1])
                kb.stt("dve", r2[b3], r2[b3], W2s[:, i:i + 1], r1[b3], ALU.mult, ALU.add,
                       [r_r2[b3], r_r1[b3], r_route], [r_r2[b3]])
                kb.stt("dve", yb[b], xt[b], ALPHA, r2[b3], ALU.mult, ALU.add, [r_xt[b], r_r2[b3]], [r_y[b]])
                kb.layernorm(yb[b], r_y[b], g4b, b4b, r_gb4, xo[b], r_xo[b], lt[b])
                kb.dma("act", out[i * 128:(i + 1) * 128, :], xo[b], [r_xo[b]], [])
        nops = P.emit()
    return nc, nops


def slot_geom(T):
    S = 768 if T >= 4096 else 256
    KMAX = -(-T // S)
    NSLOT = (2 * T + NE * (S - 1)) // S
    return S, KMAX, NSLOT


def _bf(a):
    return np.asarray(a, np.float32).astype(ml_dtypes.bfloat16).astype(np.float32)


def make_consts(T):
    NT, NB = T // 128, T // 256
    c = {}
    c["c_ident"] = np.eye(128, dtype=np.float32)
    kk = np.arange(128)[:, None]
    qq = np.arange(128)[None, :]
    tri = np.where(kk > qq, NEG, 0.0).astype(np.float32)
    tri2 = np.where(kk <= qq, NEG, 0.0).astype(np.float32)
    c["c_tri"] = np.tile(tri, (1, 4))
    c["c_tri2"] = np.tile(tri2, (1, 4))
    slopes = np.exp2(-8.0 * np.arange(1, H + 1, dtype=np.float64) / H)
    m_hi = _bf(slopes)
    m_lo = _bf(slopes - m_hi.astype(np.float64))
    pos = np.arange(T)
    pl, ph = (pos % 128).astype(np.float32), (pos // 128).astype(np.float32)
    kaug = np.zeros((H, 32, T), np.float32)
    qaug = np.zeros((H, 32, T), np.float32)
    for n in range(min(NB, 16)):
        kaug[:, n, :] = (pos // 256 == n).astype(np.float32)[None, :]
    for h in range(H):
        kaug[h, 16] = pl
        kaug[h, 17] = pl
        kaug[h, 18] = ph
        kaug[h, 19] = ph
        kaug[h, 20] = -m_hi[h]
        kaug[h, 21] = -m_lo[h]
        kaug[h, 22] = -128.0 * m_hi[h]
        kaug[h, 23] = -128.0 * m_lo[h]
        qaug[h, 16] = m_hi[h]
        qaug[h, 17] = m_lo[h]
        qaug[h, 18] = 128.0 * m_hi[h]
        qaug[h, 19] = 128.0 * m_lo[h]
        qaug[h, 20] = pl
        qaug[h, 21] = pl
        qaug[h, 22] = ph
        qaug[h, 23] = ph
    c["c_kaug"] = kaug
    c["c_qaug"] = qaug
    past = np.zeros((NT, NB), np.float32)
    own = np.zeros((NT, NB), np.float32)
    for i in range(NT):
        b = i // 2
        past[i, b:] = BIGNEG
        own[i, b] = 1.0
    c["c_past"] = np.tile(past.reshape(1, NT * NB), (128, 1))
    c["c_own"] = np.tile(own.reshape(1, NT * NB), (128, 1))
    sl = np.arange(128, dtype=np.float32)
    kaug2 = np.zeros((2, 32, 128), np.float32)
    for role in range(2):
        kaug2[role, 0] = sl
        kaug2[role, 1] = sl
        kaug2[role, 2] = 1.0
        kaug2[role, 3] = 1.0
        kaug2[role, 4] = float(role)
        kaug2[role, 5] = float(role)
    qaug2 = np.zeros((H, 32, 128), np.float32)
    for h in range(H):
        a = -(m_hi[h].astype(np.float64) + m_lo[h].astype(np.float64)) * np.arange(128, dtype=np.float64)
        a_hi = _bf(a)
        a_lo = _bf(a - a_hi.astype(np.float64))
        qaug2[h, 0] = m_hi[h]
        qaug2[h, 1] = m_lo[h]
        qaug2[h, 2] = a_hi
        qaug2[h, 3] = a_lo
        qaug2[h, 4] = -128.0 * m_hi[h]
        qaug2[h, 5] = -128.0 * m_lo[h]
    c["c_kaug2"] = kaug2
    c["c_qaug2"] = qaug2
    return c


def make_route_consts(T, FE):
    S, KMAX, NSLOT = slot_geom(T)
    NSC = FE // 512
    c = {}
    kk = np.arange(128)[:, None]
    qq = np.arange(128)[None, :]
    c["c_lstrict"] = (kk < qq).astype(np.float32)
    c["c_ones"] = np.ones((128, 128), np.float32)
    thr = np.tile((np.arange(KMAX, dtype=np.float32) * S)[None, :], (NE, 1)).reshape(1, NE * KMAX)
    c["c_thr"] = np.tile(thr, (128, 1))
    sidx = np.tile(np.arange(NSLOT, dtype=np.float32)[:, None], (1, NE)).reshape(1, NSLOT * NE)
    c["c_sidx"] = np.tile(sidx, (128, 1))
    p = np.arange(128, dtype=np.float32)[:, None]
    cc = np.arange(8, dtype=np.float32)[None, :, None]
    sc = np.arange(NSC, dtype=np.float32)[None, None, :]
    c["c_gp"] = ((cc * 128 + p[:, :, None]) * NSC + sc).reshape(128, 8 * NSC).astype(np.float32)
    sc2 = np.arange(NSC, dtype=np.float32)[None, :, None]
    c4 = np.arange(4, dtype=np.float32)[None, None, :]
    c["c_dp"] = (sc2 * 512 + c4 * 128 + p[:, :, None]).reshape(128, NSC * 4).astype(np.float32)
    return c


def make_in_maps(inputs, n_cores):
    f = lambda a: np.ascontiguousarray(np.asarray(a, dtype=np.float32))
    x = f(inputs["x"])
    T = x.shape[1]
    shared = {
        "w_qkv": f(inputs["w_qkv_a"])[0],
        "w_o_a": f(inputs["w_o_a"])[0],
        "w_kv": f(inputs["w_kv_shared"]),
        "w_q_b": f(inputs["w_q_b"])[0],
        "w_o_b": f(inputs["w_o_b"])[0],
        "sinks": f(inputs["sinks_b"]).reshape(1, H),
        "w_gate_d": f(inputs["w_gate_d"])[0],
        "w_up_d": f(inputs["w_up_d"])[0],
        "w_down_d": f(inputs["w_down_d"])[0],
        "w_router": f(inputs["w_router"])[0],
        "w_gate_e": f(inputs["w_gate_e"])[0],
        "w_up_e": f(inputs["w_up_e"])[0],
        "w_down_e": f(inputs["w_down_e"])[0],
        "ln_gain": f(inputs["ln_gain"]).reshape(4, D),
        "ln_bias": f(inputs["ln_bias"]).reshape(4, D),
    }
    shared.update(make_consts(T))
    shared.update(make_route_consts(T, shared["w_gate_e"].shape[-1]))
    maps = []
    for b in range(n_cores):
        m = dict(shared)
        m["x"] = np.ascontiguousarray(x[b])
        maps.append(m)
    return maps


def kernel(**inputs):
    x = np.asarray(inputs["x"])
    B, T, _ = x.shape
    cfg = {"T": T, "FF": np.asarray(inputs["w_gate_d"]).shape[-1], "FE": np.asarray(inputs["w_gate_e"]).shape[-1]}
    nc, _ = build(cfg)
    in_maps = make_in_maps(inputs, B)
    res = run_bass_kernel_spmd(nc, in_maps, core_ids=list(range(B)))
    return np.stack([np.asarray(r["out"], dtype=np.float32) for r in res.results], axis=0)
```

```python
import math
from contextlib import ExitStack

import numpy as np
import ml_dtypes

import concourse.bass as bass
import concourse.mybir as mybir
from concourse.bass_utils import run_bass_kernel_spmd

F32 = mybir.dt.float32
BF16 = mybir.dt.bfloat16
AF = mybir.ActivationFunctionType
ALU = mybir.AluOpType
AX = mybir.AxisListType

D = 1024
H = 16
HD = 64
NE = 8
NEG = -30000.0
BIGNEG = -1.0e9
ALPHA = 4.0 ** 0.25
LN_EPS = 1e-5

SEM_LIMIT = 20000
N_DMA_SEMS = 16


class Res:
    __slots__ = ("writers", "readers", "excl")

    def __init__(self, excl=False):
        self.writers = []
        self.readers = []
        self.excl = excl


class Op:
    __slots__ = ("eng", "fn", "deps", "signal", "sem_key", "sem_val", "is_dma")

    def __init__(self, eng, fn, is_dma):
        self.eng = eng
        self.fn = fn
        self.deps = []
        self.signal = False
        self.sem_key = None
        self.sem_val = None
        self.is_dma = is_dma


class Prog:
    ENGS = ("pe", "act", "dve", "pool", "sp")

    def __init__(self, nc):
        self.nc = nc
        self.ops = []
        self.dma_count = {e: 0 for e in self.ENGS}
        self.dma_last = {}
        self.last_compute = {}
        self.pending = {}
        self.dead = False

    def _record(self, op, reads, writes):
        deps = {}
        for r in reads:
            seen = set()
            for w in reversed(r.writers):
                if not w.is_dma:
                    if w.eng in seen:
                        continue
                    seen.add(w.eng)
                deps[id(w)] = (w, "raw")
            if r.excl and op.eng in ("act", "dve"):
                for rd in r.readers:
                    if rd.eng != op.eng and rd.eng in ("act", "dve"):
                        deps[id(rd)] = (rd, "raw")
        for w in writes:
            for ww in w.writers:
                if id(ww) not in deps:
                    deps[id(ww)] = (ww, "waw")
            for rd in w.readers:
                if id(rd) not in deps:
                    deps[id(rd)] = (rd, "war")
        for p, kind in deps.values():
            if p is op:
                continue
            if (not p.is_dma) and (not op.is_dma) and p.eng == op.eng:
                if op.eng == "pe" or kind == "waw":
                    continue
            if p.is_dma and op.is_dma and p.eng == op.eng and kind == "waw":
                continue
            op.deps.append(p)
        pend = self.pending.get(op.eng)
        if pend:
            for p in pend:
                if p is not op and p not in op.deps:
                    op.deps.append(p)
            self.pending[op.eng] = None
        for r in reads:
            r.readers.append(op)
        for w in writes:
            if w.readers:
                w.writers = [op]
                w.readers = []
            else:
                w.writers.append(op)
                if len(w.writers) > 40:
                    w.writers = w.writers[-40:]
        if not op.is_dma:
            self.last_compute[op.eng] = op
        self.ops.append(op)
        return op

    def op(self, eng, fn, reads=(), writes=()):
        if self.dead:
            return None
        return self._record(Op(eng, fn, False), reads, writes)

    def dma(self, eng, out_ap, in_ap, reads=(), writes=(), fn=None):
        if fn is None:
            def fn(e, out_ap=out_ap, in_ap=in_ap):
                return e.dma_start(out=out_ap, in_=in_ap)

        if self.dead:
            return None
        op = Op(eng, fn, True)
        k = self.dma_count[eng]
        self.dma_count[eng] = k + 1
        slot = k % N_DMA_SEMS
        op.sem_key = ("dma", eng, slot)
        op.sem_val = 16 * (k // N_DMA_SEMS + 1)
        op.signal = True
        prev = self.dma_last.get((eng, slot))
        self._record(op, reads, writes)
        if prev is not None and prev not in op.deps:
            op.deps.append(prev)
        self.dma_last[(eng, slot)] = op
        return op

    def barrier(self):
        lasts = list(self.last_compute.values()) + list(self.dma_last.values())
        for e in self.ENGS:
            self.pending[e] = list(lasts)

    def emit(self, final_eng="sp"):
        nc = self.nc
        for op in self.ops:
            for p in op.deps:
                if not p.is_dma:
                    p.signal = True
        cnt = {e: 0 for e in self.ENGS}
        epoch = {e: 0 for e in self.ENGS}
        keys = set()
        for op in self.ops:
            if op.is_dma:
                keys.add(op.sem_key)
                continue
            if op.signal:
                if cnt[op.eng] >= SEM_LIMIT:
                    cnt[op.eng] = 0
                    epoch[op.eng] += 1
                cnt[op.eng] += 1
                op.sem_key = ("eng", op.eng, epoch[op.eng])
                op.sem_val = cnt[op.eng]
                keys.add(op.sem_key)
        by_eng = {e: [] for e in self.ENGS}
        for op in self.ops:
            by_eng[op.eng].append(op)
        final = {}
        for op in self.ops:
            if op.is_dma:
                final[op.sem_key] = max(final.get(op.sem_key, 0), op.sem_val)
        with ExitStack() as es:
            sems = {}
            for key in sorted(keys, key=str):
                sems[key] = es.enter_context(nc.semaphore("s_" + "_".join(str(x) for x in key)))
            block = es.enter_context(nc.Block())

            def run(engname, e):
                known = {}
                for op in by_eng[engname]:
                    need = {}
                    for p in op.deps:
                        if need.get(p.sem_key, 0) < p.sem_val:
                            need[p.sem_key] = p.sem_val
                    for key, val in need.items():
                        if known.get(key, 0) >= val:
                            continue
                        e.wait_ge(sems[key], val)
                        known[key] = val
                    ins = op.fn(e)
                    if op.signal:
                        ins.then_inc(sems[op.sem_key], 16 if op.is_dma else 1)
                if engname == final_eng:
                    for key, val in final.items():
                        if known.get(key, 0) < val:
                            e.wait_ge(sems[key], val)

            @block.tensor
            def _(e):
                run("pe", e)

            @block.scalar
            def _(e):
                run("act", e)

            @block.vector
            def _(e):
                run("dve", e)

            @block.gpsimd
            def _(e):
                run("pool", e)

            @block.sync
            def _(e):
                run("sp", e)
        return len(self.ops)


ARENA_COLS = 103000


class Deferred:
    def __init__(self, la):
        self.q = []
        self.la = la

    def push(self, fn):
        self.q.append(fn)

    def step(self):
        while len(self.q) > self.la:
            self.q.pop(0)()

    def flush(self):
        while self.q:
            self.q.pop(0)()


class Delayed:
    def __init__(self):
        self.q = []

    def push(self, fn, delay):
        self.q.append([delay, fn])

    def tick(self):
        for it in self.q:
            it[0] -= 1
        while self.q and self.q[0][0] <= 0:
            self.q.pop(0)[1]()

    def flush(self):
        while self.q:
            self.q.pop(0)[1]()


class KB:
    def cut(self, n):
        if self.cfg.get("cut") == n:
            self.P.dead = True

    def __init__(self, nc, cfg):
        self.nc = nc
        self.cfg = cfg
        self.P = Prog(nc)
        self.off = 0
        self.persist = 0

    def alloc(self, cols, dt):
        n = cols * 2 if dt == F32 else cols
        n = (n + 15) // 16 * 16
        a = self.arena[:, self.off:self.off + n]
        self.off += n
        assert self.off <= ARENA_COLS, f"SBUF arena overflow {self.off}"
        if dt == F32:
            return a.bitcast(F32)[:, 0:cols]
        return a[:, 0:cols]

    def release(self):
        self.off = self.persist
        self.P.barrier()

    def mm(self, out, lhsT, rhs, start, stop, rd, wr):
        self.P.op("pe", lambda e: e.matmul(out, lhsT, rhs, start=start, stop=stop), rd, wr)

    def tr(self, out, in_, ident, rd, wr):
        self.P.op("pe", lambda e: e.transpose(out, in_, ident), rd, wr)

    def act(self, out, in_, func, rd, wr, scale=1.0, bias=None):
        if bias is None:
            self.P.op("act", lambda e: e.activation(out=out, in_=in_, func=func, scale=scale), rd, wr)
        else:
            self.P.op("act", lambda e: e.activation(out=out, in_=in_, func=func, bias=bias, scale=scale), rd, wr)

    def cp(self, eng, out, in_, rd, wr):
        if eng == "act":
            self.P.op("act", lambda e: e.activation(out=out, in_=in_, func=AF.Copy), rd, wr)
        else:
            self.P.op(eng, lambda e: e.tensor_copy(out=out, in_=in_), rd, wr)

    def tt(self, eng, out, in0, in1, op, rd, wr):
        self.P.op(eng, lambda e: e.tensor_tensor(out=out, in0=in0, in1=in1, op=op), rd, wr)

    def ts(self, eng, out, in0, s1, s2, op0, op1, rd, wr):
        if s2 is None:
            self.P.op(eng, lambda e: e.tensor_scalar(out=out, in0=in0, scalar1=s1, scalar2=None, op0=op0), rd, wr)
        else:
            self.P.op(eng, lambda e: e.tensor_scalar(out=out, in0=in0, scalar1=s1, scalar2=s2, op0=op0, op1=op1), rd, wr)

    def stt(self, eng, out, in0, scalar, in1, op0, op1, rd, wr):
        self.P.op(eng, lambda e: e.scalar_tensor_tensor(out=out, in0=in0, scalar=scalar, in1=in1, op0=op0, op1=op1), rd, wr)

    def red(self, out, in_, op, rd, wr):
        self.P.op("dve", lambda e: e.tensor_reduce(out=out, in_=in_, axis=AX.X, op=op), rd, wr)

    def recip(self, out, in_, rd, wr):
        self.P.op("dve", lambda e: e.reciprocal(out=out, in_=in_), rd, wr)

    def memset(self, eng, ap, val, wr):
        self.P.op(eng, lambda e: e.memset(ap, val), (), wr)

    def dma(self, eng, out, in_, rd=(), wr=()):
        self.P.dma(eng, out, in_, rd, wr)

    def gather(self, out, table, idx, rd=(), wr=()):
        def fn(e):
            return e.indirect_dma_start(out=out, out_offset=None, in_=table,
                                        in_offset=bass.IndirectOffsetOnAxis(ap=idx, axis=0))
        self.P.dma("pool", None, None, rd, wr, fn=fn)

    def scatter(self, table, idx, in_, rd=(), wr=()):
        def fn(e):
            return e.indirect_dma_start(out=table, out_offset=bass.IndirectOffsetOnAxis(ap=idx, axis=0),
                                        in_=in_, in_offset=None)
        self.P.dma("pool", None, None, rd, wr, fn=fn)

    def layernorm(self, y, r_y, gain_bc, bias_bc, r_gb, out, r_out, tmp):
        st, mv, sd, rstd, xn, r_t = tmp
        self.P.op("dve", lambda e: e.bn_stats(out=st[:, 0:6], in_=y[:, 0:512]), [r_y], [r_t[0]])
        self.P.op("dve", lambda e: e.bn_stats(out=st[:, 6:12], in_=y[:, 512:1024]), [r_y], [r_t[0]])
        self.P.op("dve", lambda e: e.bn_aggr(out=mv, in_=st.rearrange("p (a b) -> p a b", b=6)), [r_t[0]], [r_t[1]])
        self.act(sd, mv[:, 1:2], AF.Sqrt, [r_t[1]], [r_t[2]], bias=LN_EPS)
        self.recip(rstd, sd, [r_t[2]], [r_t[3]])
        self.stt("dve", sd, mv[:, 0:1], -1.0, rstd, ALU.mult, ALU.mult, [r_t[1], r_t[3], r_t[2]], [r_t[2]])
        self.act(xn, y, AF.Identity, [r_y, r_t[2], r_t[3]], [r_t[4]], scale=rstd[:, 0:1], bias=sd[:, 0:1])
        self.tt("dve", xn, xn, gain_bc, ALU.mult, [r_t[4], r_gb], [r_t[4]])
        self.tt("dve", out, xn, bias_bc, ALU.add, [r_t[4], r_gb], [r_out])

    def ln_tmp(self):
        st = self.alloc(12, F32)
        mv = self.alloc(2, F32)
        sd = self.alloc(1, F32)
        rstd = self.alloc(1, F32)
        xn = self.alloc(D, F32)
        return (st, mv, sd, rstd, xn, [Res() for _ in range(5)])


def build(cfg):
    T = cfg["T"]
    FF = cfg["FF"]
    FE = cfg["FE"]
    dbg = cfg.get("debug", False)
    phases = cfg.get("phases", "ABCD")
    NT, NG, NB = T // 128, T // 512, T // 256
    assert T % 512 == 0 and NT * NB <= 512
    nc = bass.Bass("TRN2", target_bir_lowering=False)

    def din(name, shape):
        return nc.dram_tensor(name, shape, F32, kind="ExternalInput").ap()

    def dscr(name, shape, dt):
        return nc.dram_tensor(name, shape, dt, kind=("ExternalOutput" if dbg else "Internal")).ap()

    x = din("x", [T, D])
    w_qkv = din("w_qkv", [D, 3 * D])
    w_o_a = din("w_o_a", [D, D])
    w_kv = din("w_kv", [D, 256])
    w_q_b = din("w_q_b", [D, D])
    w_o_b = din("w_o_b", [D, D])
    sinks = din("sinks", [1, H])
    w_gate_d = din("w_gate_d", [D, FF])
    w_up_d = din("w_up_d", [D, FF])
    w_down_d = din("w_down_d", [FF, D])
    w_router = din("w_router", [D, NE])
    w_gate_e = din("w_gate_e", [NE, D, FE])
    w_up_e = din("w_up_e", [NE, D, FE])
    w_down_e = din("w_down_e", [NE, FE, D])
    ln_gain = din("ln_gain", [4, D])
    ln_bias = din("ln_bias", [4, D])
    c_ident = din("c_ident", [128, 128])
    c_tri = din("c_tri", [128, 512])
    c_tri2 = din("c_tri2", [128, 512])
    c_kaug = din("c_kaug", [H, 32, T])
    c_qaug = din("c_qaug", [H, 32, T])
    c_past = din("c_past", [128, NT * NB])
    c_own = din("c_own", [128, NT * NB])
    c_kaug2 = din("c_kaug2", [2, 32, 128])
    c_qaug2 = din("c_qaug2", [H, 32, 128])
    S, KMAX, NSLOT = slot_geom(T)
    NSCR = FE // 512
    c_lstrict = din("c_lstrict", [128, 128])
    c_ones = din("c_ones", [128, 128])
    c_thr = din("c_thr", [128, NE * KMAX])
    c_sidx = din("c_sidx", [128, NSLOT * NE])
    c_gp = din("c_gp", [128, 8 * NSCR])
    c_dp = din("c_dp", [128, NSCR * 4])
    out = nc.dram_tensor("out", [T, D], F32, kind="ExternalOutput").ap()

    attnT = dscr("attnT", [D, T], BF16)
    x1s = dscr("x1s", [T, D], F32)
    x1Ts = dscr("x1Ts", [NT, 128, D], BF16)
    x2s = dscr("x2s", [T, D], F32)
    x2Ts = dscr("x2Ts", [NT, 128, D], BF16)
    x3s = dscr("x3s", [T, D], F32)
    x3Ts = dscr("x3Ts", [NT, 128, D], BF16)
    gates_d = dscr("gates_d", [128, NT * NE], F32)
    x3bs = dscr("x3bs", [T, D], BF16)

    with ExitStack() as es:
        kb = KB(nc, cfg)
        kb.arena = es.enter_context(nc.sbuf_tensor("arena", [128, ARENA_COLS], BF16))
        ps = [es.enter_context(nc.psum_tensor(f"ps{i}", [128, 512], F32)) for i in range(8)]
        psb = [p.bitcast(BF16) for p in ps]
        r_ps = [Res(excl=True) for _ in range(8)]
        P = kb.P
        alloc = kb.alloc

        ident = alloc(128, BF16)
        identf = alloc(128, F32)
        tri = alloc(512, BF16)
        tri2 = alloc(512, BF16)
        onesf = alloc(64, F32)
        gates = alloc(NT * NE, F32)
        E1s = alloc(NT * NE, F32)
        E2s = alloc(NT * NE, F32)
        W1s = alloc(NT, F32)
        W2s = alloc(NT, F32)
        r_route = Res()
        r_c = Res()
        r_gates = Res()
        kb.dma("pool", ident, c_ident, wr=[r_c])
        kb.dma("sp", identf, c_ident, wr=[r_c])
        kb.dma("pool", tri, c_tri, wr=[r_c])
        kb.dma("pool", tri2, c_tri2, wr=[r_c])
        kb.memset("pool", onesf, 1.0, [r_c])
        xsorted = dscr("xsorted", [NSLOT * S, D], BF16)
        ysd = dscr("ysd", [NSLOT * S, D], F32)
        r_xso = Res()
        if "D" in phases and cfg.get("moe", "routed") == "routed":
            zt = alloc(D, BF16)
            r_zt = Res()
            kb.memset("pool", zt, 0.0, [r_zt])
        kb.persist = kb.off

        def load_gb(idx):
            g = alloc(D, F32)
            b = alloc(D, F32)
            r = Res()
            kb.dma("sp", g, ln_gain[idx:idx + 1, :].to_broadcast([128, D]), wr=[r])
            kb.dma("sp", b, ln_bias[idx:idx + 1, :].to_broadcast([128, D]), wr=[r])
            return g, b, r

        def t_cast(src_f32, r_src, xb, r_xb):
            kb.cp("act", xb, src_f32, [r_src], [r_xb])

        def to_T_block(src_f32, r_src, dst_dram, bank, xb, r_xb, xTt, r_xTt, r_dst, copy_eng, r_bank=None, store_q="pool",
                       do_cast=True):
            if r_bank is None:
                r_bank = r_ps[bank]
            if do_cast:
                t_cast(src_f32, r_src, xb, r_xb)
            pst = psb[bank].rearrange("p (c t) -> p c t", c=8)
            for c in range(8):
                kb.tr(pst[:, c, :], xb[:, c * 128:(c + 1) * 128], ident, [r_xb, r_c], [r_bank])
            kb.cp(copy_eng, xTt.rearrange("p (c t) -> p c t", c=8), pst, [r_bank], [r_xTt])
            kb.dma(store_q, dst_dram, xTt, [r_xTt], [r_dst])

        r_attn = [Res() for _ in range(NG)]
        if "A" in phases:
            xT = alloc(8 * T, BF16).rearrange("p (c t) -> p c t", c=8)
            r_xT = [Res() for _ in range(NT)]
            past = alloc(NT * NB, F32)
            own = alloc(NT * NB, F32)
            kb.dma("sp", past, c_past, wr=[r_c])
            kb.dma("sp", own, c_own, wr=[r_c])
            xbuf = [alloc(D, BF16) for _ in range(4)]
            r_xbuf = [Res() for _ in range(4)]
            for i in range(0 if cfg.get("skipA0") else NT):
                b = i % 2
                b4 = i % 4
                kb.dma("pool", xbuf[b4], x[i * 128:(i + 1) * 128, :], wr=[r_xbuf[b4]])
                pst = psb[b].rearrange("p (c t) -> p c t", c=8)
                for c in range(8):
                    kb.tr(pst[:, c, :], xbuf[b4][:, c * 128:(c + 1) * 128], ident, [r_xbuf[b4], r_c], [r_ps[b]])
                kb.cp("act" if i % 2 else "dve", xT[:, :, i * 128:(i + 1) * 128], pst, [r_ps[b]], [r_xT[i]])

            kb.cut(1)
            wh = [alloc(8 * 192, BF16).rearrange("p (c n) -> p c n", c=8) for _ in range(2)]
            r_wh = [Res(), Res()]
            kTb = [alloc(T, BF16) for _ in range(2)]
            qTb = [alloc(T, BF16) for _ in range(2)]
            r_k = [[Res() for _ in range(NG)] for _ in range(2)]
            r_q = [[Res() for _ in range(NG)] for _ in range(2)]
            Vb = [alloc(NT * 65, BF16).rearrange("p (i d) -> p i d", d=65) for _ in range(2)]
            NVB = (NT + 7) // 8
            r_v = [[Res() for _ in range(NVB)] for _ in range(2)]
            for b in range(2):
                kb.memset("pool", Vb[b][:, :, 64:65], 1.0, r_v[b])
            qf = alloc(T, F32)
            r_qf = [Res() for _ in range(NG)]
            km = alloc(NB, F32)
            r_km = Res()
            mbp = alloc(NT * 80, BF16).rearrange("p (i m) -> p i m", m=80)
            r_mbp = Res()
            kb.memset("pool", mbp, 0.0, [r_mbp])
            tmp = [alloc(NT * NB, F32) for _ in range(4)]
            r_tmp = [Res() for _ in range(4)]
            mx = [alloc(NT, F32) for _ in range(3)]
            r_mx = [Res() for _ in range(3)]
            Pt = [alloc(512, BF16) for _ in range(3)]
            r_pt = [Res() for _ in range(3)]
            rec = alloc(512, F32)
            r_rec = Res()
            otmp = alloc(512, F32)
            r_otmp = Res()
            on_ = [alloc(512, BF16) for _ in range(2)]
            r_on = [Res(), Res()]

            r_ser = Res()

            def v3(ap):
                return ap.rearrange("p (i n) -> p i n", n=NB)

            def prep(h):
                b = h % 2
                for sec in range(3):
                    kb.dma("pool", wh[b][:, :, sec * 64:(sec + 1) * 64],
                           w_qkv[:, sec * D + h * 64: sec * D + (h + 1) * 64].rearrange("(c p) n -> p c n", p=128),
                           wr=[r_wh[b]])
                kb.dma("pool", kTb[b][64:96, :], c_kaug[h], wr=r_k[b])
                kb.dma("pool", qTb[b][64:96, :], c_qaug[h], wr=r_q[b])
                kb.cut(21)
                yield
                for g in range(NG):
                    pb = g % 2 + cfg.get("kbank", 0)
                    pk = ps[pb][0:64, :]
                    for c in range(8):
                        kb.mm(pk, wh[b][:, c, 64:128], xT[:, c, g * 512:(g + 1) * 512], c == 0, c == 7,
                              [r_wh[b]] + r_xT[4 * g:4 * g + 4], [r_ps[pb]])
                    kb.red(km[0:64, 2 * g:2 * g + 2], pk.rearrange("p (a s) -> p a s", s=256), ALU.add,
                           [r_ps[pb]], [r_km, r_ser])
                    kb.cp("dve", kTb[b][0:64, g * 512:(g + 1) * 512], pk, [r_ps[pb], r_ser], [r_k[b][g]])
                    yield
                kb.cut(22)
                for g in range(NG):
                    pb = g % 2
                    pq = ps[pb][0:64, :]
                    for c in range(8):
                        kb.mm(pq, wh[b][:, c, 0:64], xT[:, c, g * 512:(g + 1) * 512], c == 0, c == 7,
                              [r_wh[b]] + r_xT[4 * g:4 * g + 4], [r_ps[pb]])
                    kb.ts("dve", qf[0:64, g * 512:(g + 1) * 512], pq, 0.125, None, ALU.mult, None,
                          [r_ps[pb]], [r_qf[g], r_ser])
                    kb.ts("dve", qTb[b][0:64, g * 512:(g + 1) * 512], pq, 0.125, None, ALU.mult, None,
                          [r_ps[pb], r_ser], [r_q[b][g]])
                    yield
                kb.cut(23)
                for i in range(NT):
                    pb = (i // 8) % 2
                    pv = ps[pb][:, (i % 8) * 64:(i % 8 + 1) * 64]
                    for c in range(8):
                        kb.mm(pv, xT[:, c, i * 128:(i + 1) * 128], wh[b][:, c, 128:192], c == 0, c == 7,
                              [r_wh[b], r_xT[i]], [r_ps[pb]])
                    if i % 8 == 7 or i == NT - 1:
                        n = i % 8 + 1
                        base = i - n + 1
                        kb.cp("dve", Vb[b][:, base:base + n, 0:64],
                              ps[pb][:, 0:n * 64].rearrange("p (i d) -> p i d", d=64), [r_ps[pb]], [r_v[b][i // 8]])
                        yield
                kb.cut(24)
                pg = ps[0][:, 0:NT * NB]
                for i in range(NT):
                    kb.mm(pg[:, i * NB:(i + 1) * NB], qf[0:64, i * 128:(i + 1) * 128], km[0:64, 0:NB], True, True,
                          [r_qf[i // 4], r_km], [r_ps[0]])
                yield
                kb.cut(25)
                g0, g1, g2, e1 = tmp
                kb.tt("dve", g0, pg, past, ALU.add, [r_ps[0], r_c], [r_tmp[0]])
                kb.red(mx[0], v3(g0), ALU.max, [r_tmp[0]], [r_mx[0]])
                kb.tt("dve", v3(e1), v3(g0), mx[0].unsqueeze(2).to_broadcast([128, NT, NB]), ALU.is_ge,
                      [r_tmp[0], r_mx[0]], [r_tmp[3]])
                kb.stt("dve", g1, e1, BIGNEG, g0, ALU.mult, ALU.add, [r_tmp[3], r_tmp[0]], [r_tmp[1]])
                kb.red(mx[1], v3(g1), ALU.max, [r_tmp[1]], [r_mx[1]])
                kb.tt("dve", v3(e1), v3(g1), mx[1].unsqueeze(2).to_broadcast([128, NT, NB]), ALU.is_ge,
                      [r_tmp[1], r_mx[1]], [r_tmp[3]])
                kb.stt("dve", g2, e1, BIGNEG, g1, ALU.mult, ALU.add, [r_tmp[3], r_tmp[1]], [r_tmp[2]])
                kb.red(mx[2], v3(g2), ALU.max, [r_tmp[2]], [r_mx[2]])
                kb.tt("dve", v3(e1), v3(g0), mx[2].unsqueeze(2).to_broadcast([128, NT, NB]), ALU.is_ge,
                      [r_tmp[0], r_mx[2]], [r_tmp[3]])
                kb.tt("dve", g1, e1, own, ALU.max, [r_tmp[3], r_c], [r_tmp[1]])
                kb.ts("dve", mbp[:, :, 64:64 + NB], v3(g1), -1.0, -NEG, ALU.add, ALU.mult, [r_tmp[1]], [r_mbp])
                yield
                kb.cut(26)
                for i in range(NT):
                    kb.mm(ps[1][0:80, (i % 4) * 128:(i % 4 + 1) * 128], mbp[:, i, :], ident, True, True,
                          [r_mbp, r_c], [r_ps[1]])
                    if i % 4 == 3:
                        kb.cp("dve", qTb[b][64:64 + NB, (i - 3) * 128:(i + 1) * 128], ps[1][64:64 + NB, 0:512],
                              [r_ps[1]], [r_q[b][i // 4]])
                        yield

            cnt = [0]

            def attend(h, nxt=None):
                b = h % 2
                dq = Deferred(2)
                dqn = Delayed()
                tcount = 0
                for g in range(NG):
                    ob = 5 + (g % 2)
                    OT = ps[ob]
                    nj = 4 * g + 4
                    for j in range(nj):
                        lo = 0 if j < 4 * g else 128 * (j - 4 * g)
                        k = cnt[0] % 3
                        cnt[0] += 1
                        sb_ = 2 + k
                        kb.mm(ps[sb_][:, lo:512], kTb[b][0:96, j * 128:(j + 1) * 128],
                              qTb[b][0:96, g * 512 + lo:(g + 1) * 512], True, j < 4 * g,
                              [r_k[b][j // 4], r_q[b][g]], [r_ps[sb_]])
                        if j >= 4 * g:
                            kb.mm(ps[sb_][:, lo:lo + 128], ident, tri[:, 0:128], False, True, [r_c], [r_ps[sb_]])
                        kb.act(Pt[k][:, lo:512], ps[sb_][:, lo:512], AF.Exp, [r_ps[sb_]], [r_pt[k]])

                        def pv(OT=OT, ob=ob, j=j, lo=lo, k=k, nj=nj):
                            kb.mm(OT[0:65, lo:512], Vb[b][:, j, 0:65], Pt[k][:, lo:512], j == 0, j == nj - 1,
                                  [r_v[b][j // 8], r_pt[k]], [r_ps[ob]])

                        dq.push(pv)
                        if j == nj - 1:
                            def norm1(OT=OT, ob=ob):
                                kb.act(rec[64:65, :], OT[64:65, :], AF.Ln, [r_ps[ob]], [r_rec])
                                kb.act(rec[64:65, :], rec[64:65, :], AF.Exp, [r_rec], [r_rec], scale=-1.0)
                                kb.cp("dve", otmp[0:64, :], OT[0:64, :], [r_ps[ob]], [r_otmp])

                            def norm2(g=g):
                                kb.mm(ps[7][0:64, :], onesf[64:65, 0:64], rec[64:65, :], True, True, [r_rec, r_c], [r_ps[7]])
                                k2 = g % 2
                                kb.tt("dve", on_[k2][0:64, :], otmp[0:64, :], ps[7][0:64, :], ALU.mult,
                                      [r_otmp, r_ps[7]], [r_on[k2]])
                                kb.dma("sp", attnT[h * 64:(h + 1) * 64, g * 512:(g + 1) * 512], on_[k2][0:64, :],
                                       [r_on[k2]], [r_attn[g]])

                            dq.push(norm1)
                            dqn.push(norm2, 4)
                        dq.step()
                        dqn.tick()
                        tcount += 1
                        if nxt is not None and tcount % 4 == 0:
                            next(nxt, None)
                dq.flush()
                dqn.flush()

            NHh = cfg.get("nheads", H)
            zf = [r for r in range(NSLOT * (S // 128))] if ("D" in phases and cfg.get("moe", "routed") == "routed") else []
            for _ in prep(0):
                pass
            for h in range(NHh):
                nxt = prep(h + 1) if h + 1 < NHh else None
                for _ in range(-(-len(zf) // max(1, NHh - h))):
                    r = zf.pop(0)
                    kb.dma("sp", xsorted[r * 128:(r + 1) * 128, :], zt, [r_zt], [r_xso])
                attend(h, nxt)
                if nxt is not None:
                    for _ in nxt:
                        pass
            kb.release()
            kb.cut(4)

        r_x1 = [Res() for _ in range(NT)]
        r_x1T = [Res() for _ in range(NT)]
        r_x2 = [Res() for _ in range(NT)]
        r_x2T = [Res() for _ in range(NT)]
        if "B" in phases:
            NFC = FF // 128
            Wg = alloc(8 * FF, BF16).rearrange("p (c n) -> p c n", c=8)
            Wu = alloc(8 * FF, BF16).rearrange("p (c n) -> p c n", c=8)
            r_wgu = Res()
            saved_persist = kb.persist
            kb.persist = kb.off
            Wo = alloc(8 * D, BF16).rearrange("p (c n) -> p c n", c=8)
            r_w = Res()
            kb.dma("pool", Wo, w_o_a.rearrange("(c p) n -> p c n", p=128), wr=[r_w])
            wgu_pieces = []
            g1b, b1b, r_gb1 = load_gb(0)
            aT = [alloc(8 * 512, BF16).rearrange("p (c t) -> p c t", c=8) for _ in range(2)]
            r_aT = [Res(), Res()]
            xt = [alloc(D, F32) for _ in range(2)]
            r_xt = [Res(), Res()]
            yb = [alloc(D, F32) for _ in range(2)]
            r_y = [Res(), Res()]
            x1 = [alloc(D, F32) for _ in range(2)]
            r_x1b = [Res(), Res()]
            xb16 = [alloc(D, BF16) for _ in range(2)]
            r_xb16 = [Res(), Res()]
            xTt = [alloc(D, BF16) for _ in range(2)]
            r_xTt = [Res(), Res()]
            lt = [kb.ln_tmp() for _ in range(2)]
            dqb = Deferred(1)
            for g in range(NG):
                gb = g % 2
                kb.dma("sp", aT[gb], attnT[:, g * 512:(g + 1) * 512].rearrange("(c p) t -> p c t", p=128),
                       [r_attn[g]], [r_aT[gb]])
                for t in range(4):
                    i = 4 * g + t
                    b = i % 2
                    npc = -(-len(wgu_pieces) // max(1, NT - 2))
                    for _ in range(npc if cfg.get("prefetch_wgu", False) else 0):
                        if wgu_pieces:
                            dst_, src_ = wgu_pieces.pop(0)
                            kb.dma("pool", dst_, src_, wr=[r_wgu])
                    for n in range(2):
                        for c in range(8):
                            kb.mm(ps[2 + n][:, :], aT[gb][:, c, t * 128:(t + 1) * 128], Wo[:, c, n * 512:(n + 1) * 512],
                                  c == 0, c == 7, [r_aT[gb], r_w], [r_ps[2 + n]])
                    kb.dma("sp", xt[b], x[i * 128:(i + 1) * 128, :], wr=[r_xt[b]])
                    for n in range(2):
                        kb.stt("dve", yb[b][:, n * 512:(n + 1) * 512], xt[b][:, n * 512:(n + 1) * 512], ALPHA,
                               ps[2 + n][:, :], ALU.mult, ALU.add, [r_xt[b], r_ps[2 + n]], [r_y[b]])
                    kb.layernorm(yb[b], r_y[b], g1b, b1b, r_gb1, x1[b], r_x1b[b], lt[b])
                    kb.dma("pool", x1s[i * 128:(i + 1) * 128, :], x1[b], [r_x1b[b]], [r_x1[i]])
                    t_cast(x1[b], r_x1b[b], xb16[b], r_xb16[b])

                    def tb(i=i, b=b):
                        to_T_block(x1[b], r_x1b[b], x1Ts[i], b, xb16[b], r_xb16[b], xTt[b], r_xTt[b], r_x1T[i],
                                   "act" if i % 2 else "dve", do_cast=False)

                    dqb.push(tb)
                    dqb.step()
            dqb.flush()
            kb.release()
            kb.cut(5)
            while wgu_pieces:
                dst_, src_ = wgu_pieces.pop(0)
                kb.dma("pool", dst_, src_, wr=[r_wgu])

            Wd = alloc(NFC * D, BF16).rearrange("p (c n) -> p c n", c=NFC)
            NPW = (NFC + 2) // 3
            r_wp = [Res() for _ in range(NPW)]
            wg_src = w_gate_d.rearrange("(c p) n -> p c n", p=128)
            wu_src = w_up_d.rearrange("(c p) n -> p c n", p=128)
            wd_src = w_down_d.rearrange("(c p) n -> p c n", p=128)
            for j in range(NPW):
                c0, c1 = 3 * j, min(NFC, 3 * j + 3)
                kb.dma("pool", Wg[:, :, c0 * 128:c1 * 128], wg_src[:, :, c0 * 128:c1 * 128], wr=[r_wp[j]])
                kb.dma("pool", Wu[:, :, c0 * 128:c1 * 128], wu_src[:, :, c0 * 128:c1 * 128], wr=[r_wp[j]])
                kb.dma("pool", Wd[:, c0:c1, :], wd_src[:, c0:c1, :], wr=[r_wp[j]])
            g2b, b2b, r_gb2 = load_gb(1)
            xg = [alloc(8 * 256, BF16).rearrange("p (c t) -> p c t", c=8) for _ in range(2)]
            r_xg = [Res(), Res()]
            sg = [alloc(256, F32) for _ in range(2)]
            r_sg = [Res(), Res()]
            hT = [alloc(256, BF16) for _ in range(2)]
            r_hT = [Res(), Res()]
            xt = [alloc(D, F32) for _ in range(2)]
            r_xt = [Res(), Res()]
            yb = [alloc(D, F32) for _ in range(2)]
            r_y = [Res(), Res()]
            x2 = [alloc(D, F32) for _ in range(2)]
            r_x2b = [Res(), Res()]
            xb16 = [alloc(D, BF16) for _ in range(2)]
            r_xb16 = [Res(), Res()]
            xTt = [alloc(D, BF16) for _ in range(2)]
            r_xTt = [Res(), Res()]
            lt = [kb.ln_tmp() for _ in range(2)]
            r_g = [Res(excl=True), Res(excl=True)]
            r_u = [Res(excl=True), Res(excl=True)]
            dq = Deferred(1)
            for tg in range(NT // 2):
                gb = tg % 2
                for t in range(2):
                    kb.dma("sp", xg[gb][:, :, t * 128:(t + 1) * 128],
                           x1Ts[2 * tg + t].rearrange("p (c t) -> p c t", c=8), [r_x1T[2 * tg + t]], [r_xg[gb]])
                for c in range(NFC):
                    cb = c % 2
                    pgu = ps[cb]
                    for k in range(8):
                        kb.mm(pgu[:, 0:256], Wg[:, k, c * 128:(c + 1) * 128], xg[gb][:, k, :], k == 0, k == 7,
                              [r_wp[c // 3], r_xg[gb]], [r_g[cb]])
                    for k in range(8):
                        kb.mm(ps[6 + cb][:, 0:256], Wu[:, k, c * 128:(c + 1) * 128], xg[gb][:, k, :], k == 0, k == 7,
                              [r_wp[c // 3], r_xg[gb]], [r_u[cb]])
                    kb.act(sg[cb], pgu[:, 0:256], AF.Silu, [r_g[cb]], [r_sg[cb]])
                    kb.tt("dve", hT[cb], sg[cb], ps[6 + cb][:, 0:256], ALU.mult, [r_sg[cb], r_u[cb]], [r_hT[cb]])

                    def down(c=c, cb=cb):
                        for t in range(2):
                            for n in range(2):
                                bank = 2 + 2 * t + n
                                kb.mm(ps[bank][:, :], hT[cb][:, t * 128:(t + 1) * 128], Wd[:, c, n * 512:(n + 1) * 512],
                                      c == 0, c == NFC - 1, [r_hT[cb], r_wp[c // 3]], [r_ps[bank]])

                    dq.push(down)
                    dq.step()

                def fin(tg=tg):
                    for t in range(2):
                        i = 2 * tg + t
                        b = i % 2
                        kb.dma("sp", xt[b], x1s[i * 128:(i + 1) * 128, :], [r_x1[i]], [r_xt[b]])
                        for n in range(2):
                            bank = 2 + 2 * t + n
                            kb.stt("dve", yb[b][:, n * 512:(n + 1) * 512], xt[b][:, n * 512:(n + 1) * 512], ALPHA,
                                   ps[bank][:, :], ALU.mult, ALU.add, [r_xt[b], r_ps[bank]], [r_y[b]])
                        kb.layernorm(yb[b], r_y[b], g2b, b2b, r_gb2, x2[b], r_x2b[b], lt[b])
                        kb.dma("pool", x2s[i * 128:(i + 1) * 128, :], x2[b], [r_x2b[b]], [r_x2[i]])
                        t_cast(x2[b], r_x2b[b], xb16[b], r_xb16[b])

                def fin2(tg=tg):
                    for t in range(2):
                        i = 2 * tg + t
                        b = i % 2
                        to_T_block(x2[b], r_x2b[b], x2Ts[i], b, xb16[b], r_xb16[b], xTt[b], r_xTt[b], r_x2T[i],
                                   "act" if i % 2 else "dve", r_bank=r_g[b], do_cast=False)

                dq.push(fin)
                dq.push(fin2)
            dq.flush()
            kb.persist = saved_persist
            kb.release()

        r_x3 = [Res() for _ in range(NT)]
        r_x3T = [Res() for _ in range(NT)]
        if "C" in phases:
            Wq2 = alloc(8 * D, BF16).rearrange("p (c n) -> p c n", c=8)
            Wkv = alloc(8 * 256, BF16).rearrange("p (c n) -> p c n", c=8)
            Wo2 = alloc(H * D, BF16).rearrange("p (h n) -> p h n", h=H)
            wr = alloc(8 * NE, F32).rearrange("p (c e) -> p c e", c=8)
            r_w = Res()
            kb.dma("pool", Wq2, w_q_b.rearrange("(c p) n -> p c n", p=128), wr=[r_w])
            kb.dma("pool", Wkv, w_kv.rearrange("(c p) n -> p c n", p=128), wr=[r_w])
            kb.dma("pool", Wo2[0:64, :, :], w_o_b.rearrange("(h p) n -> p h n", p=64), wr=[r_w])
            kb.dma("sp", wr, w_router.rearrange("(c p) e -> p c e", p=128), wr=[r_w])
            g3b, b3b, r_gb3 = load_gb(2)
            sk = alloc(H, F32)
            esk = alloc(H * 128, F32)
            ones64 = alloc(64, BF16)
            r_sk = Res()
            kb.dma("sp", sk[0:64, :], sinks.to_broadcast([64, H]), wr=[r_sk])
            kb.act(sk[0:64, :], sk[0:64, :], AF.Exp, [r_sk], [r_sk])
            kb.cp("dve", esk[0:64, :].rearrange("p (h t) -> p h t", t=128),
                  sk[0:64, :].unsqueeze(2).to_broadcast([64, H, 128]), [r_sk], [r_sk])
            kb.memset("pool", ones64, 1.0, [r_sk])
            kcur = alloc(2 * 8 * 128, BF16).rearrange("p (k s t) -> p k s t", k=2, s=8)
            kprev = alloc(2 * 8 * 128, BF16).rearrange("p (k s t) -> p k s t", k=2, s=8)
            r_kc = [Res(), Res()]
            V2 = alloc(8 * 2 * 65, BF16).rearrange("p (s k d) -> p s k d", s=8, k=2)
            r_v2 = [Res(), Res()]
            kb.memset("pool", V2[:, :, :, 64:65], 1.0, r_v2)
            for kv in range(2):
                for s in range(8):
                    kb.dma("pool", kcur[64:96, kv, s, :], c_kaug2[0], wr=r_kc)
                    kb.dma("pool", kprev[64:96, kv, s, :], c_kaug2[1], wr=r_kc)
            q2 = alloc(4 * H * 128, BF16).rearrange("p (t h s) -> p t h s", t=4, h=H)
            r_q2 = Res()
            for t in range(4):
                kb.dma("pool", q2[64:96, t, :, :], c_qaug2.rearrange("h r s -> r h s"), wr=[r_q2])
            on2 = alloc(4 * H * 128, BF16).rearrange("p (t h s) -> p t h s", t=4, h=H)
            r_on2 = [Res() for _ in range(4)]
            xg = [alloc(8 * 512, BF16).rearrange("p (c t) -> p c t", c=8) for _ in range(2)]
            r_xg = [Res(), Res()]
            Pc = [alloc(512, BF16) for _ in range(2)]
            Pp = [alloc(512, BF16) for _ in range(2)]
            r_pc = [Res(), Res()]
            r_pp = [Res(), Res()]
            den = alloc(512, F32)
            rec = alloc(512, F32)
            r_den = Res()
            r_rec = Res()
            otmp = alloc(512, F32)
            r_otmp = Res()
            xt = [alloc(D, F32) for _ in range(2)]
            r_xt = [Res(), Res()]
            yb = [alloc(D, F32) for _ in range(2)]
            r_y = [Res(), Res()]
            x3 = [alloc(D, F32) for _ in range(2)]
            r_x3b = [Res(), Res()]
            xb16 = [alloc(D, BF16) for _ in range(2)]
            r_xb16 = [Res(), Res()]
            xTt = [alloc(D, BF16) for _ in range(2)]
            r_xTt = [Res(), Res()]
            x3Tf = alloc(D, F32).rearrange("p (c t) -> p c t", c=8)
            r_x3Tf = Res()
            lg = [alloc(NE, F32) for _ in range(4)]
            r_lg = [Res() for _ in range(4)]
            sm = [alloc(1, F32) for _ in range(6)]
            r_sm = [Res() for _ in range(6)]
            lt = [kb.ln_tmp() for _ in range(2)]
            cc = 0
            dq = Deferred(1)
            for g in range(NG):
                gb = g % 2
                so = (g % 2) * 4
                for t in range(4):
                    kb.dma("sp", xg[gb][:, :, t * 128:(t + 1) * 128],
                           x2Ts[4 * g + t].rearrange("p (c t) -> p c t", c=8), [r_x2T[4 * g + t]], [r_xg[gb]])
                for kv in range(2):
                    pk = ps[kv][0:64, :]
                    for c in range(8):
                        kb.mm(pk, Wkv[:, c, kv * 64:(kv + 1) * 64], xg[gb][:, c, :], c == 0, c == 7,
                              [r_w, r_xg[gb]], [r_ps[kv]])
                    pk3 = pk.rearrange("p (s t) -> p s t", t=128)
                    kb.cp("act", kcur[0:64, kv, so:so + 4, :], pk3, [r_ps[kv]], [r_kc[gb]])
                    kb.cp("dve", kprev[0:64, kv, so:so + 4, :], pk3, [r_ps[kv]], [r_kc[gb]])
                for t in range(4):
                    for c in range(8):
                        kb.mm(ps[2][:, t * 128:(t + 1) * 128], xg[gb][:, c, t * 128:(t + 1) * 128], Wkv[:, c, 128:256],
                              c == 0, c == 7, [r_w, r_xg[gb]], [r_ps[2]])
                kb.cp("dve", V2[:, so:so + 4, :, 0:64], ps[2][:, :].rearrange("p (s k d) -> p s k d", s=4, k=2),
                      [r_ps[2]], [r_v2[gb]])
                for h in range(H):
                    pb = h % 2
                    pq = ps[pb][0:64, :]
                    for c in range(8):
                        kb.mm(pq, Wq2[:, c, h * 64:(h + 1) * 64], xg[gb][:, c, :], c == 0, c == 7,
                              [r_w, r_xg[gb]], [r_ps[pb]])
                    pq3 = pq.rearrange("p (t s) -> p t s", s=128)
                    if h % 2:
                        kb.ts("dve", q2[0:64, :, h, :], pq3, 0.125, None, ALU.mult, None, [r_ps[pb]], [r_q2])
                    else:
                        kb.act(q2[0:64, :, h, :], pq3, AF.Copy, [r_ps[pb]], [r_q2], scale=0.125)
                for t in range(4):
                    i = 4 * g + t
                    sl = so + t
                    psl = (sl - 1) % 8
                    rgp = (gb if t > 0 else 1 - gb)
                    for kv in range(2):
                        for half in range(2):
                            hs = kv * 8 + half * 4
                            rhs = q2[0:96, t, hs:hs + 4, :]
                            k = cc % 2
                            cc += 1
                            bc_, bp_ = 2 + k, 4 + k
                            kb.mm(ps[bc_][:, :], kcur[0:96, kv, sl, :], rhs, True, False, [r_kc[gb], r_q2], [r_ps[bc_]])
                            kb.mm(ps[bc_][:, :], ident, tri, False, True, [r_c], [r_ps[bc_]])
                            kb.act(Pc[k], ps[bc_][:, :], AF.Exp, [r_ps[bc_]], [r_pc[k]])
                            if i > 0:
                                kb.mm(ps[bp_][:, :], kprev[0:96, kv, psl, :], rhs, True, False, [r_kc[rgp], r_q2], [r_ps[bp_]])
                                kb.mm(ps[bp_][:, :], ident, tri2, False, True, [r_c], [r_ps[bp_]])
                                kb.act(Pp[k], ps[bp_][:, :], AF.Exp, [r_ps[bp_]], [r_pp[k]])

                            def pv(i=i, t=t, kv=kv, hs=hs, k=k, sl=sl, psl=psl, rgp=rgp, gb=gb):
                                kb.mm(ps[6][0:64, :], V2[:, sl, kv, 0:64], Pc[k], True, i == 0, [r_v2[gb], r_pc[k]], [r_ps[6]])
                                if i > 0:
                                    kb.mm(ps[6][0:64, :], V2[:, psl, kv, 0:64], Pp[k], False, True, [r_v2[rgp], r_pp[k]], [r_ps[6]])
                                kb.mm(ps[7][0:64, :], ones64[:, 0:64], Pc[k], True, i == 0, [r_sk, r_pc[k]], [r_ps[7]])
                                if i > 0:
                                    kb.mm(ps[7][0:64, :], ones64[:, 0:64], Pp[k], False, True, [r_sk, r_pp[k]], [r_ps[7]])
                                kb.tt("dve", den[0:64, :], ps[7][0:64, :], esk[0:64, hs * 128:(hs + 4) * 128], ALU.add,
                                      [r_ps[7], r_sk], [r_den])
                                kb.act(den[0:64, :], den[0:64, :], AF.Ln, [r_den], [r_den])
                                kb.act(rec[0:64, :], den[0:64, :], AF.Exp, [r_den], [r_rec], scale=-1.0)
                                kb.tt("dve", on2[0:64, t, hs:hs + 4, :], ps[6][0:64, :].rearrange("p (h s) -> p h s", s=128),
                                      rec[0:64, :].rearrange("p (h s) -> p h s", s=128), ALU.mult,
                                      [r_ps[6], r_rec], [r_on2[t]])

                            dq.push(pv)
                            dq.step()

                    def tilefin(i=i, t=t):
                        b = i % 2
                        for n in range(2):
                            for h in range(H):
                                kb.mm(ps[n][:, :], on2[0:64, t, h, :], Wo2[0:64, h, n * 512:(n + 1) * 512], h == 0, h == H - 1,
                                      [r_on2[t], r_w], [r_ps[n]])
                        kb.dma("sp", xt[b], x2s[i * 128:(i + 1) * 128, :], [r_x2[i]], [r_xt[b]])
                        for n in range(2):
                            kb.stt("dve", yb[b][:, n * 512:(n + 1) * 512], xt[b][:, n * 512:(n + 1) * 512], ALPHA,
                                   ps[n][:, :], ALU.mult, ALU.add, [r_xt[b], r_ps[n]], [r_y[b]])
                        kb.layernorm(yb[b], r_y[b], g3b, b3b, r_gb3, x3[b], r_x3b[b], lt[b])
                        kb.dma("pool", x3s[i * 128:(i + 1) * 128, :], x3[b], [r_x3b[b]], [r_x3[i]])
                        t_cast(x3[b], r_x3b[b], xb16[b], r_xb16[b])
                        kb.dma("pool", x3bs[i * 128:(i + 1) * 128, :], xb16[b], [r_xb16[b]], [r_x3[i]])

                    def tilefin2(i=i, t=t):
                        b = i % 2
                        to_T_block(x3[b], r_x3b[b], x3Ts[i], 0, xb16[b], r_xb16[b], xTt[b], r_xTt[b], r_x3T[i], "dve",
                                   do_cast=False)
                        for half in range(2):
                            pT = ps[half][:, :].rearrange("p (c t) -> p c t", c=4)
                            for c in range(4):
                                cc_ = half * 4 + c
                                kb.tr(pT[:, c, :], x3[b][:, cc_ * 128:(cc_ + 1) * 128], identf, [r_x3b[b], r_c], [r_ps[half]])
                            kb.cp("dve" if half else "act", x3Tf[:, half * 4:(half + 1) * 4, :], pT, [r_ps[half]], [r_x3Tf])
                        for c in range(8):
                            kb.mm(ps[1][:, 0:NE], x3Tf[:, c, :], wr[:, c, :], c == 0, c == 7, [r_x3Tf, r_w], [r_ps[1]])
                        l0, l1, e1_, e2_ = lg
                        m1, m2, dl, ex, w1, w2 = sm
                        E1i = E1s[:, i * NE:(i + 1) * NE]
                        E2i = E2s[:, i * NE:(i + 1) * NE]
                        kb.cp("dve", l0, ps[1][:, 0:NE], [r_ps[1]], [r_lg[0]])
                        kb.red(m1, l0, ALU.max, [r_lg[0]], [r_sm[0]])
                        kb.ts("dve", E1i, l0, m1[:, 0:1], None, ALU.is_ge, None, [r_lg[0], r_sm[0]], [r_lg[2], r_route])
                        kb.stt("dve", l1, E1i, BIGNEG, l0, ALU.mult, ALU.add, [r_lg[2], r_lg[0]], [r_lg[1]])
                        kb.red(m2, l1, ALU.max, [r_lg[1]], [r_sm[1]])
                        kb.ts("dve", E2i, l1, m2[:, 0:1], None, ALU.is_ge, None, [r_lg[1], r_sm[1]], [r_lg[3], r_route])
                        kb.tt("dve", dl, m2, m1, ALU.subtract, [r_sm[0], r_sm[1]], [r_sm[2]])
                        kb.act(ex, dl, AF.Exp, [r_sm[2]], [r_sm[3]])
                        kb.ts("dve", w1, ex, 1.0, None, ALU.add, None, [r_sm[3]], [r_sm[4]])
                        kb.recip(W1s[:, i:i + 1], w1, [r_sm[4]], [r_sm[5], r_route])
                        kb.tt("dve", W2s[:, i:i + 1], ex, W1s[:, i:i + 1], ALU.mult, [r_sm[3], r_sm[5]], [r_sm[5], r_route])
                        kb.ts("dve", e1_, E1i, W1s[:, i:i + 1], None, ALU.mult, None, [r_lg[2], r_sm[5]], [r_lg[2]])
                        kb.stt("dve", gates[:, i * NE:(i + 1) * NE], E2i, W2s[:, i:i + 1], e1_, ALU.mult, ALU.add,
                               [r_lg[3], r_sm[5], r_lg[2]], [r_gates])

                    dq.push(tilefin)
                    dq.push(tilefin2)
            dq.flush()
            if dbg:
                kb.dma("sp", gates_d, gates, [r_gates], [])
            kb.release()

        if "D" in phases and cfg.get("moe", "routed") == "dense":
            NSC = FE // 512
            NHALF = 2 if NT >= 4 else 1
            HT = NT // NHALF
            g4b, b4b, r_gb4 = load_gb(3)
            acc = alloc(HT * D, F32).rearrange("p (i n) -> p i n", n=D)
            r_acc = [Res() for _ in range(HT)]
            x3T = alloc(8 * HT * 128, BF16).rearrange("p (c t) -> p c t", c=8)
            r_x3Tr = [Res() for _ in range(HT)]
            wg = [alloc(8 * 512, BF16).rearrange("p (c n) -> p c n", c=8) for _ in range(2)]
            wu = [alloc(8 * 512, BF16).rearrange("p (c n) -> p c n", c=8) for _ in range(2)]
            wd = [alloc(4 * D, BF16).rearrange("p (c n) -> p c n", c=4) for _ in range(2)]
            r_wb = [Res(), Res()]
            sg = [alloc(256, F32) for _ in range(2)]
            r_sg = [Res(), Res()]
            hT = [alloc(256, BF16) for _ in range(2)]
            r_hT = [Res(), Res()]
            xt = [alloc(D, F32) for _ in range(2)]
            r_xt = [Res(), Res()]
            yb = [alloc(D, F32) for _ in range(2)]
            r_y = [Res(), Res()]
            xo = [alloc(D, F32) for _ in range(2)]
            r_xo = [Res(), Res()]
            lt = [kb.ln_tmp() for _ in range(2)]
            wcnt = 0
            ccnt = 0
            r_g = [Res(excl=True), Res(excl=True)]
            r_u = [Res(excl=True), Res(excl=True)]
            dq = Deferred(1)
            for hf in range(NHALF):
                t0 = hf * HT
                for it in range(HT):
                    kb.dma("sp", x3T[:, :, it * 128:(it + 1) * 128], x3Ts[t0 + it].rearrange("p (c t) -> p c t", c=8),
                           [r_x3T[t0 + it]], [r_x3Tr[it]])
                    kb.memset("pool", acc[:, it, :], 0.0, [r_acc[it]])
                for e in range(NE):
                    for sc in range(NSC):
                        wb = wcnt % 2
                        wcnt += 1
                        kb.dma("pool", wg[wb], w_gate_e[e, :, sc * 512:(sc + 1) * 512].rearrange("(c p) n -> p c n", p=128),
                               wr=[r_wb[wb]])
                        kb.dma("pool", wu[wb], w_up_e[e, :, sc * 512:(sc + 1) * 512].rearrange("(c p) n -> p c n", p=128),
                               wr=[r_wb[wb]])
                        kb.dma("pool", wd[wb], w_down_e[e, sc * 512:(sc + 1) * 512, :].rearrange("(c p) n -> p c n", p=128),
                               wr=[r_wb[wb]])
                        for tg in range(HT // 2):
                            xs = x3T[:, :, tg * 256:(tg + 1) * 256]
                            rx = [r_x3Tr[2 * tg], r_x3Tr[2 * tg + 1]]
                            for c in range(4):
                                cb = ccnt % 2
                                ccnt += 1
                                pgu = ps[cb]
                                for k in range(8):
                                    kb.mm(pgu[:, 0:256], wg[wb][:, k, c * 128:(c + 1) * 128], xs[:, k, :], k == 0, k == 7,
                                          [r_wb[wb]] + rx, [r_g[cb]])
                                for k in range(8):
                                    kb.mm(ps[6 + cb][:, 0:256], wu[wb][:, k, c * 128:(c + 1) * 128], xs[:, k, :], k == 0, k == 7,
                                          [r_wb[wb]] + rx, [r_u[cb]])
                                kb.act(sg[cb], pgu[:, 0:256], AF.Silu, [r_g[cb]], [r_sg[cb]])
                                kb.tt("dve", hT[cb], sg[cb], ps[6 + cb][:, 0:256], ALU.mult, [r_sg[cb], r_u[cb]], [r_hT[cb]])

                                def down(c=c, cb=cb, wb=wb):
                                    for t in range(2):
                                        for n in range(2):
                                            bank = 2 + 2 * t + n
                                            kb.mm(ps[bank][:, :], hT[cb][:, t * 128:(t + 1) * 128],
                                                  wd[wb][:, c, n * 512:(n + 1) * 512],
                                                  c == 0, c == 3, [r_hT[cb], r_wb[wb]], [r_ps[bank]])

                                dq.push(down)
                                dq.step()

                            def accupd(tg=tg, e=e):
                                for t in range(2):
                                    it = 2 * tg + t
                                    gi = t0 + it
                                    for n in range(2):
                                        bank = 2 + 2 * t + n
                                        kb.stt("dve", acc[:, it, n * 512:(n + 1) * 512], ps[bank][:, :],
                                               gates[:, gi * NE + e:gi * NE + e + 1], acc[:, it, n * 512:(n + 1) * 512],
                                               ALU.mult, ALU.add, [r_ps[bank], r_gates, r_acc[it]], [r_acc[it]])

                            dq.push(accupd)
                dq.flush()
                for it in range(HT):
                    gi = t0 + it
                    b = it % 2
                    kb.dma("sp", xt[b], x3s[gi * 128:(gi + 1) * 128, :], [r_x3[gi]], [r_xt[b]])
                    kb.stt("dve", yb[b], xt[b], ALPHA, acc[:, it, :], ALU.mult, ALU.add, [r_xt[b], r_acc[it]], [r_y[b]])
                    kb.layernorm(yb[b], r_y[b], g4b, b4b, r_gb4, xo[b], r_xo[b], lt[b])
                    kb.dma("sp", out[gi * 128:(gi + 1) * 128, :], xo[b], [r_xo[b]], [])
        if "D" in phases and cfg.get("moe", "routed") == "routed":
            NSC = NSCR
            TS = S // 128
            I32 = mybir.dt.int32
            wg_flat = w_gate_e.rearrange("e d (n c) -> (e d n) c", c=512)
            wu_flat = w_up_e.rearrange("e d (n c) -> (e d n) c", c=512)
            wd_flat = w_down_e.rearrange("e f d -> (e f) d")
            g4b, b4b, r_gb4 = load_gb(3)
            lst = alloc(128, F32)
            onesm = alloc(128, F32)
            thr = alloc(NE * KMAX, F32)
            sidx = alloc(NSLOT * NE, F32)
            gp = alloc(8 * NSC, F32)
            dp = alloc(NSC * 4, F32)
            r_k = Res()
            kb.dma("sp", lst, c_lstrict, wr=[r_k])
            kb.dma("sp", onesm, c_ones, wr=[r_k])
            kb.dma("sp", thr, c_thr, wr=[r_k])
            kb.dma("sp", sidx, c_sidx, wr=[r_k])
            kb.dma("sp", gp, c_gp, wr=[r_k])
            kb.dma("sp", dp, c_dp, wr=[r_k])
            Mm = alloc(NT * NE, F32)
            Msum = alloc((NT + 1) * NE, F32)
            Rk = alloc(NT * NE, F32)
            ntot = alloc(NE, F32)
            cmp1 = alloc(NE * KMAX, F32)
            ns = alloc(NE, F32)
            cs = alloc(NE, F32)
            cend = alloc(NE, F32)
            offs = alloc(NE, F32)
            tmp3 = alloc(NT * NE, F32)
            tmp4 = alloc(NT * NE, F32)
            pos1f = alloc(NT, F32)
            pos2f = alloc(NT, F32)
            pos1i = alloc(NT, I32 if False else F32).bitcast(I32)
            pos2i = alloc(NT, F32).bitcast(I32)
            cmp2 = alloc(NSLOT * NE, F32)
            esf = alloc(NSLOT, F32)
            A3 = alloc(NSLOT * 8 * NSC, F32)
            idxG = alloc(NSLOT * 8 * NSC, F32).bitcast(I32)
            B3 = alloc(NSLOT * NSC * 4, F32)
            idxD = alloc(NSLOT * NSC * 4, F32).bitcast(I32)
            r_t = [Res() for _ in range(16)]
            r_idx = Res()
            kb.tt("dve", Mm, E1s, E2s, ALU.add, [r_route], [r_t[0]])
            kb.memset("dve", Msum[:, 0:NE], 0.0, [r_t[1]])
            for i in range(NT):
                kb.tt("dve", Msum[:, (i + 1) * NE:(i + 2) * NE], Msum[:, i * NE:(i + 1) * NE], Mm[:, i * NE:(i + 1) * NE],
                      ALU.add, [r_t[0], r_t[1]], [r_t[1]])
            for i in range(NT):
                kb.mm(ps[0][:, i * NE:(i + 1) * NE], lst, Mm[:, i * NE:(i + 1) * NE], True, False, [r_k, r_t[0]], [r_ps[0]])
                kb.mm(ps[0][:, i * NE:(i + 1) * NE], onesm, Msum[:, i * NE:(i + 1) * NE], False, True, [r_k, r_t[1]], [r_ps[0]])
            kb.mm(ps[1][:, 0:NE], onesm, Msum[:, NT * NE:(NT + 1) * NE], True, True, [r_k, r_t[1]], [r_ps[1]])
            kb.cp("dve", Rk, ps[0][:, 0:NT * NE], [r_ps[0]], [r_t[2]])
            kb.cp("dve", ntot, ps[1][:, 0:NE], [r_ps[1]], [r_t[3]])
            kb.tt("dve", cmp1.rearrange("p (e k) -> p e k", k=KMAX), ntot.unsqueeze(2).to_broadcast([128, NE, KMAX]),
                  thr.rearrange("p (e k) -> p e k", k=KMAX), ALU.is_gt, [r_t[3], r_k], [r_t[4]])
            kb.red(ns, cmp1.rearrange("p (e k) -> p e k", k=KMAX), ALU.add, [r_t[4]], [r_t[5]])
            kb.memset("dve", cs[:, 0:1], 0.0, [r_t[6]])
            for e in range(1, NE):
                kb.tt("dve", cs[:, e:e + 1], cs[:, e - 1:e], ns[:, e - 1:e], ALU.add, [r_t[5], r_t[6]], [r_t[6]])
            kb.tt("dve", cend, cs, ns, ALU.add, [r_t[5], r_t[6]], [r_t[7]])
            kb.ts("dve", offs, cs, float(S), None, ALU.mult, None, [r_t[6]], [r_t[8]])
            kb.tt("dve", tmp3.rearrange("p (i e) -> p i e", e=NE), Rk.rearrange("p (i e) -> p i e", e=NE),
                  offs.unsqueeze(1).to_broadcast([128, NT, NE]), ALU.add, [r_t[2], r_t[8]], [r_t[9]])
            kb.tt("dve", tmp4, tmp3, E1s, ALU.mult, [r_t[9], r_route], [r_t[10]])
            kb.red(pos1f, tmp4.rearrange("p (i e) -> p i e", e=NE), ALU.add, [r_t[10]], [r_t[11]])
            kb.cp("dve", pos1i, pos1f, [r_t[11]], [r_idx])
            kb.tt("dve", tmp4, tmp3, E2s, ALU.mult, [r_t[9], r_route], [r_t[10]])
            kb.red(pos2f, tmp4.rearrange("p (i e) -> p i e", e=NE), ALU.add, [r_t[10]], [r_t[11]])
            kb.cp("dve", pos2i, pos2f, [r_t[11]], [r_idx])
            kb.tt("dve", cmp2.rearrange("p (s e) -> p s e", e=NE), cend.unsqueeze(1).to_broadcast([128, NSLOT, NE]),
                  sidx.rearrange("p (s e) -> p s e", e=NE), ALU.is_le, [r_t[7], r_k], [r_t[12]])
            kb.red(esf, cmp2.rearrange("p (s e) -> p s e", e=NE), ALU.add, [r_t[12]], [r_t[13]])
            kb.ts("dve", esf, esf, float(NE - 1), None, ALU.min, None, [r_t[13]], [r_t[13]])
            kb.cp("dve", A3.rearrange("p (s j) -> p s j", s=NSLOT), esf.unsqueeze(2).to_broadcast([128, NSLOT, 8 * NSC]),
                  [r_t[13]], [r_t[14]])
            kb.stt("dve", A3.rearrange("p (s j) -> p s j", s=NSLOT), A3.rearrange("p (s j) -> p s j", s=NSLOT), float(D * NSC),
                   gp.unsqueeze(1).to_broadcast([128, NSLOT, 8 * NSC]), ALU.mult, ALU.add, [r_t[14], r_k], [r_t[14]])
            kb.cp("dve", idxG, A3, [r_t[14]], [r_idx])
            kb.cp("dve", B3.rearrange("p (s j) -> p s j", s=NSLOT), esf.unsqueeze(2).to_broadcast([128, NSLOT, NSC * 4]),
                  [r_t[13]], [r_t[15]])
            kb.stt("dve", B3.rearrange("p (s j) -> p s j", s=NSLOT), B3.rearrange("p (s j) -> p s j", s=NSLOT), float(FE),
                   dp.unsqueeze(1).to_broadcast([128, NSLOT, NSC * 4]), ALU.mult, ALU.add, [r_t[15], r_k], [r_t[15]])
            kb.cp("dve", idxD, B3, [r_t[15]], [r_idx])
            xb = [alloc(D, BF16) for _ in range(4)]
            r_xb = [Res() for _ in range(4)]
            for i in range(NT):
                b = i % 4
                kb.dma("sp", xb[b], x3bs[i * 128:(i + 1) * 128, :], [r_x3[i]], [r_xb[b]])
                kb.scatter(xsorted[:, :], pos1i[:, i:i + 1], xb[b], [r_xb[b], r_idx], [r_xso])
                kb.scatter(xsorted[:, :], pos2i[:, i:i + 1], xb[b], [r_xb[b], r_idx], [r_xso])
            P.barrier()
            d2_mark = kb.off
            xsl = [alloc(D, BF16) for _ in range(2)]
            r_xsl = [Res(), Res()]
            xsT = alloc(8 * S, BF16).rearrange("p (c t) -> p c t", c=8)
            r_xsT = [Res() for _ in range(TS)]
            acc = alloc(TS * D, F32).rearrange("p (i n) -> p i n", n=D)
            r_acc = [Res() for _ in range(TS)]
            wg = [alloc(8 * 512, BF16).rearrange("p (c n) -> p c n", c=8) for _ in range(3)]
            wu = [alloc(8 * 512, BF16).rearrange("p (c n) -> p c n", c=8) for _ in range(3)]
            wd = [alloc(4 * D, BF16).rearrange("p (c n) -> p c n", c=4) for _ in range(3)]
            r_wb = [Res(), Res(), Res()]
            sg = [alloc(256, F32) for _ in range(2)]
            r_sg = [Res(), Res()]
            hT = [alloc(256, BF16) for _ in range(2)]
            r_hT = [Res(), Res()]
            r_g = [Res(excl=True), Res(excl=True)]
            r_u = [Res(excl=True), Res(excl=True)]
            r_ys = Res()
            dq = Deferred(1)
            wcnt = 0
            ccnt = 0
            for sl in range(NSLOT):
                for t in range(TS):
                    b = t % 2
                    kb.dma("sp", xsl[b], xsorted[(sl * TS + t) * 128:(sl * TS + t + 1) * 128, :], [r_xso], [r_xsl[b]])
                    pst = psb[b].rearrange("p (c t) -> p c t", c=8)
                    for c in range(8):
                        kb.tr(pst[:, c, :], xsl[b][:, c * 128:(c + 1) * 128], ident, [r_xsl[b], r_c], [r_g[b]])
                    kb.cp("act" if t % 2 else "dve", xsT[:, :, t * 128:(t + 1) * 128], pst, [r_g[b]], [r_xsT[t]])
                for sc in range(NSC):
                    wb = wcnt % 3
                    wcnt += 1
                    for c in range(8):
                        col = (sl * 8 + c) * NSC + sc
                        kb.gather(wg[wb][:, c, :], wg_flat, idxG[:, col:col + 1], [r_idx], [r_wb[wb]])
                        kb.gather(wu[wb][:, c, :], wu_flat, idxG[:, col:col + 1], [r_idx], [r_wb[wb]])
                    for c in range(4):
                        col = (sl * NSC + sc) * 4 + c
                        kb.gather(wd[wb][:, c, :], wd_flat, idxD[:, col:col + 1], [r_idx], [r_wb[wb]])
                    for tg in range(TS // 2):
                        xs_ = xsT[:, :, tg * 256:(tg + 1) * 256]
                        rx = [r_xsT[2 * tg], r_xsT[2 * tg + 1]]
                        for c in range(4):
                            cb = ccnt % 2
                            ccnt += 1
                            for k in range(8):
                                kb.mm(ps[cb][:, 0:256], wg[wb][:, k, c * 128:(c + 1) * 128], xs_[:, k, :], k == 0, k == 7,
                                      [r_wb[wb]] + rx, [r_g[cb]])
                            for k in range(8):
                                kb.mm(ps[6 + cb][:, 0:256], wu[wb][:, k, c * 128:(c + 1) * 128], xs_[:, k, :], k == 0, k == 7,
                                      [r_wb[wb]] + rx, [r_u[cb]])
                            kb.act(sg[cb], ps[cb][:, 0:256], AF.Silu, [r_g[cb]], [r_sg[cb]])
                            kb.tt("dve", hT[cb], sg[cb], ps[6 + cb][:, 0:256], ALU.mult, [r_sg[cb], r_u[cb]], [r_hT[cb]])

                            def down(c=c, cb=cb, wb=wb):
                                for t in range(2):
                                    for n in range(2):
                                        bank = 2 + 2 * t + n
                                        kb.mm(ps[bank][:, :], hT[cb][:, t * 128:(t + 1) * 128],
                                              wd[wb][:, c, n * 512:(n + 1) * 512],
                                              c == 0, c == 3, [r_hT[cb], r_wb[wb]], [r_ps[bank]])

                            dq.push(down)
                            dq.step()

                        def accupd(tg=tg, sc=sc, sl=sl):
                            for t in range(2):
                                it = 2 * tg + t
                                for n in range(2):
                                    bank = 2 + 2 * t + n
                                    if sc == 0:
                                        kb.cp("dve", acc[:, it, n * 512:(n + 1) * 512], ps[bank][:, :],
                                              [r_ps[bank]], [r_acc[it]])
                                    else:
                                        kb.tt("dve", acc[:, it, n * 512:(n + 1) * 512], ps[bank][:, :],
                                              acc[:, it, n * 512:(n + 1) * 512], ALU.add, [r_ps[bank], r_acc[it]], [r_acc[it]])
                                if sc == NSC - 1:
                                    row = (sl * TS + it) * 128
                                    kb.dma("act", ysd[row:row + 128, :], acc[:, it, :], [r_acc[it]], [r_ys])

                        dq.push(accupd)
            dq.flush()
            P.barrier()
            kb.off = d2_mark
            r1 = [alloc(D, F32) for _ in range(3)]
            r2 = [alloc(D, F32) for _ in range(3)]
            r_r1 = [Res() for _ in range(3)]
            r_r2 = [Res() for _ in range(3)]
            xt = [alloc(D, F32) for _ in range(3)]
            r_xt = [Res() for _ in range(3)]
            yb = [alloc(D, F32) for _ in range(3)]
            r_y = [Res() for _ in range(3)]
            xo = [alloc(D, F32) for _ in range(3)]
            r_xo = [Res() for _ in range(3)]
            lt = [kb.ln_tmp() for _ in range(3)]
            for i in range(NT):
                b = i % 3
                b3 = i % 3
                kb.gather(r1[b3], ysd[:, :], pos1i[:, i:i + 1], [r_idx, r_ys], [r_r1[b3]])
                kb.gather(r2[b3], ysd[:, :], pos2i[:, i:i + 1], [r_idx, r_ys], [r_r2[b3]])
                kb.dma("sp", xt[b], x3s[i * 128:(i + 1) * 128, :], [r_x3[i]], [r_xt[b]])
                kb.act(r1[b3], r1[b3], AF.Copy, [r_r1[b3], r_route], [r_r1[b3]], scale=W1s[:, i:i + 1])
                kb.stt("dve", r2[b3], r2[b3], W2s[:, i:i + 1], r1[b3], ALU.mult, ALU.add,
                       [r_r2[b3], r_r1[b3], r_route], [r_r2[b3]])
                kb.stt("dve", yb[b], xt[b], ALPHA, r2[b3], ALU.mult, ALU.add, [r_xt[b], r_r2[b3]], [r_y[b]])
                kb.layernorm(yb[b], r_y[b], g4b, b4b, r_gb4, xo[b], r_xo[b], lt[b])
                kb.dma("act", out[i * 128:(i + 1) * 128, :], xo[b], [r_xo[b]], [])
        nops = P.emit()
    return nc, nops


def slot_geom(T):
    S = 768 if T >= 4096 else 256
    KMAX = -(-T // S)
    NSLOT = (2 * T + NE * (S - 1)) // S
    return S, KMAX, NSLOT


def _bf(a):
    return np.asarray(a, np.float32).astype(ml_dtypes.bfloat16).astype(np.float32)


def make_consts(T):
    NT, NB = T // 128, T // 256
    c = {}
    c["c_ident"] = np.eye(128, dtype=np.float32)
    kk = np.arange(128)[:, None]
    qq = np.arange(128)[None, :]
    tri = np.where(kk > qq, NEG, 0.0).astype(np.float32)
    tri2 = np.where(kk <= qq, NEG, 0.0).astype(np.float32)
    c["c_tri"] = np.tile(tri, (1, 4))
    c["c_tri2"] = np.tile(tri2, (1, 4))
    slopes = np.exp2(-8.0 * np.arange(1, H + 1, dtype=np.float64) / H)
    m_hi = _bf(slopes)
    m_lo = _bf(slopes - m_hi.astype(np.float64))
    pos = np.arange(T)
    pl, ph = (pos % 128).astype(np.float32), (pos // 128).astype(np.float32)
    kaug = np.zeros((H, 32, T), np.float32)
    qaug = np.zeros((H, 32, T), np.float32)
    for n in range(min(NB, 16)):
        kaug[:, n, :] = (pos // 256 == n).astype(np.float32)[None, :]
    for h in range(H):
        kaug[h, 16] = pl
        kaug[h, 17] = pl
        kaug[h, 18] = ph
        kaug[h, 19] = ph
        kaug[h, 20] = -m_hi[h]
        kaug[h, 21] = -m_lo[h]
        kaug[h, 22] = -128.0 * m_hi[h]
        kaug[h, 23] = -128.0 * m_lo[h]
        qaug[h, 16] = m_hi[h]
        qaug[h, 17] = m_lo[h]
        qaug[h, 18] = 128.0 * m_hi[h]
        qaug[h, 19] = 128.0 * m_lo[h]
        qaug[h, 20] = pl
        qaug[h, 21] = pl
        qaug[h, 22] = ph
        qaug[h, 23] = ph
    c["c_kaug"] = kaug
    c["c_qaug"] = qaug
    past = np.zeros((NT, NB), np.float32)
    own = np.zeros((NT, NB), np.float32)
    for i in range(NT):
        b = i // 2
        past[i, b:] = BIGNEG
        own[i, b] = 1.0
    c["c_past"] = np.tile(past.reshape(1, NT * NB), (128, 1))
    c["c_own"] = np.tile(own.reshape(1, NT * NB), (128, 1))
    sl = np.arange(128, dtype=np.float32)
    kaug2 = np.zeros((2, 32, 128), np.float32)
    for role in range(2):
        kaug2[role, 0] = sl
        kaug2[role, 1] = sl
        kaug2[role, 2] = 1.0
        kaug2[role, 3] = 1.0
        kaug2[role, 4] = float(role)
        kaug2[role, 5] = float(role)
    qaug2 = np.zeros((H, 32, 128), np.float32)
    for h in range(H):
        a = -(m_hi[h].astype(np.float64) + m_lo[h].astype(np.float64)) * np.arange(128, dtype=np.float64)
        a_hi = _bf(a)
        a_lo = _bf(a - a_hi.astype(np.float64))
        qaug2[h, 0] = m_hi[h]
        qaug2[h, 1] = m_lo[h]
        qaug2[h, 2] = a_hi
        qaug2[h, 3] = a_lo
        qaug2[h, 4] = -128.0 * m_hi[h]
        qaug2[h, 5] = -128.0 * m_lo[h]
    c["c_kaug2"] = kaug2
    c["c_qaug2"] = qaug2
    return c


def make_route_consts(T, FE):
    S, KMAX, NSLOT = slot_geom(T)
    NSC = FE // 512
    c = {}
    kk = np.arange(128)[:, None]
    qq = np.arange(128)[None, :]
    c["c_lstrict"] = (kk < qq).astype(np.float32)
    c["c_ones"] = np.ones((128, 128), np.float32)
    thr = np.tile((np.arange(KMAX, dtype=np.float32) * S)[None, :], (NE, 1)).reshape(1, NE * KMAX)
    c["c_thr"] = np.tile(thr, (128, 1))
    sidx = np.tile(np.arange(NSLOT, dtype=np.float32)[:, None], (1, NE)).reshape(1, NSLOT * NE)
    c["c_sidx"] = np.tile(sidx, (128, 1))
    p = np.arange(128, dtype=np.float32)[:, None]
    cc = np.arange(8, dtype=np.float32)[None, :, None]
    sc = np.arange(NSC, dtype=np.float32)[None, None, :]
    c["c_gp"] = ((cc * 128 + p[:, :, None]) * NSC + sc).reshape(128, 8 * NSC).astype(np.float32)
    sc2 = np.arange(NSC, dtype=np.float32)[None, :, None]
    c4 = np.arange(4, dtype=np.float32)[None, None, :]
    c["c_dp"] = (sc2 * 512 + c4 * 128 + p[:, :, None]).reshape(128, NSC * 4).astype(np.float32)
    return c


def make_in_maps(inputs, n_cores):
    f = lambda a: np.ascontiguousarray(np.asarray(a, dtype=np.float32))
    x = f(inputs["x"])
    T = x.shape[1]
    shared = {
        "w_qkv": f(inputs["w_qkv_a"])[0],
        "w_o_a": f(inputs["w_o_a"])[0],
        "w_kv": f(inputs["w_kv_shared"]),
        "w_q_b": f(inputs["w_q_b"])[0],
        "w_o_b": f(inputs["w_o_b"])[0],
        "sinks": f(inputs["sinks_b"]).reshape(1, H),
        "w_gate_d": f(inputs["w_gate_d"])[0],
        "w_up_d": f(inputs["w_up_d"])[0],
        "w_down_d": f(inputs["w_down_d"])[0],
        "w_router": f(inputs["w_router"])[0],
        "w_gate_e": f(inputs["w_gate_e"])[0],
        "w_up_e": f(inputs["w_up_e"])[0],
        "w_down_e": f(inputs["w_down_e"])[0],
        "ln_gain": f(inputs["ln_gain"]).reshape(4, D),
        "ln_bias": f(inputs["ln_bias"]).reshape(4, D),
    }
    shared.update(make_consts(T))
    shared.update(make_route_consts(T, shared["w_gate_e"].shape[-1]))
    maps = []
    for b in range(n_cores):
        m = dict(shared)
        m["x"] = np.ascontiguousarray(x[b])
        maps.append(m)
    return maps


def kernel(**inputs):
    x = np.asarray(inputs["x"])
    B, T, _ = x.shape
    cfg = {"T": T, "FF": np.asarray(inputs["w_gate_d"]).shape[-1], "FE": np.asarray(inputs["w_gate_e"]).shape[-1]}
    nc, _ = build(cfg)
    in_maps = make_in_maps(inputs, B)
    res = run_bass_kernel_spmd(nc, in_maps, core_ids=list(range(B)))
    return np.stack([np.asarray(r["out"], dtype=np.float32) for r in res.results], axis=0)
```
